# Optimizing a Trainium2 kernel written in Bass

```python
import jax, jax.numpy as jnp
from jax import lax
import numpy as np

D_MODEL = 1024
BATCH = 4
SEQ = 8192
DEPTH = 4

N_A_LAYERS = DEPTH // 2
N_B_LAYERS = DEPTH - N_A_LAYERS
D_FF = 2816
CONV_WIDTH = 3
HEAD_DIM = 64
HEADS_PER_GROUP = 8
DILATED_GROUPS = ((128, 1), (512, 4), (2048, 16))
N_GROUPS = len(DILATED_GROUPS)
N_Q_HEADS = N_GROUPS * HEADS_PER_GROUP
Q_WIDTH = N_Q_HEADS * HEAD_DIM
OUT_WIDTH = HEADS_PER_GROUP * HEAD_DIM
ROPE_DIM = HEAD_DIM // 4
ROPE_THETA = 500000.0
NORM_EPS = 1e-5
FFN_RES_WEIGHT = 0.5
N_MOD = 9

kernel_name = "hybrid_shortconv_dilated_yoco_trunk"


def rms_norm(x, g):
    xf = x.astype(jnp.float32)
    y = xf * lax.rsqrt(jnp.mean(xf * xf, axis=-1, keepdims=True) + NORM_EPS)
    return (y * g.astype(jnp.float32)).astype(x.dtype)


def modulate(h, shift, scale):
    return h * (1 + scale[:, None, :]) + shift[:, None, :]


def swiglu(h, w_in, w_out):
    a, b = jnp.split(h @ w_in, 2, axis=-1)
    return (jax.nn.silu(a) * b) @ w_out


def short_conv_mixer(h, w_in, conv_w, w_out):
    b_gate, c_gate, u = jnp.split(h @ w_in, 3, axis=-1)
    v = c_gate * u
    conv = lax.conv_general_dilated(
        v, conv_w[:, None, :], window_strides=(1,), padding=[(CONV_WIDTH - 1, 0)],
        dimension_numbers=('NWC', 'WIO', 'NWC'), feature_group_count=D_MODEL)
    return (b_gate * conv) @ w_out


def rope_tables(positions):
    inv = ROPE_THETA ** (-jnp.arange(0, ROPE_DIM, 2, dtype=jnp.float32) / ROPE_DIM)
    ang = positions.astype(jnp.float32)[..., None] * inv
    return jnp.cos(ang)[:, :, None, :], jnp.sin(ang)[:, :, None, :]


def apply_partial_rope(t, cos, sin):
    tf = t.astype(jnp.float32)
    r1 = tf[..., :ROPE_DIM // 2]
    r2 = tf[..., ROPE_DIM // 2:ROPE_DIM]
    out = jnp.concatenate([r1 * cos - r2 * sin, r2 * cos + r1 * sin, tf[..., ROPE_DIM:]], axis=-1)
    return out.astype(t.dtype)


def dilated_window_attention(q, k, v, window, dilation):
    bsz, seq, nh, hd = q.shape
    n = window // dilation
    span = n * dilation
    seq_p = -(-seq // span) * span
    pad = seq_p - seq
    m_len = seq_p // dilation
    nb = m_len // n

    def to_blocks(t):
        t = jnp.pad(t, ((0, 0), (0, pad), (0, 0), (0, 0)))
        t = t.reshape(bsz, m_len, dilation, nh, hd).transpose(0, 2, 3, 1, 4)
        return t.reshape(bsz, dilation, nh, nb, n, hd)

    def with_prev(t):
        prev = jnp.pad(t[:, :, :, :-1], ((0, 0), (0, 0), (0, 0), (1, 0), (0, 0), (0, 0)))
        return jnp.concatenate([prev, t], axis=-2)

    qb = to_blocks(q)
    kk = with_prev(to_blocks(k))
    vv = with_prev(to_blocks(v))
    s = jnp.einsum('brhiqe,brhike->brhiqk', qb, kk,
                   preferred_element_type=jnp.float32) * (hd ** -0.5)
    blk = jnp.arange(nb)[:, None, None]
    qi = jnp.arange(n)[None, :, None]
    kj = jnp.arange(2 * n)[None, None, :]
    dist = n + qi - kj
    valid = (dist >= 0) & (dist <= n) & ((blk - 1) * n + kj >= 0)
    s = jnp.where(valid, s, -jnp.inf)
    m = jnp.max(s, axis=-1, keepdims=True)
    p = jnp.exp(s - m)
    den = jnp.sum(p, axis=-1, keepdims=True)
    o = jnp.einsum('brhiqk,brhike->brhiqe', (p / den).astype(v.dtype), vv)
    lse = (m + jnp.log(den))[..., 0]
    o = o.reshape(bsz, dilation, nh, m_len, hd).transpose(0, 3, 1, 2, 4).reshape(bsz, seq_p, nh, hd)
    lse = lse.reshape(bsz, dilation, nh, m_len).transpose(0, 3, 1, 2).reshape(bsz, seq_p, nh)
    return o[:, :seq], lse[:, :seq]


def dilated_attention_mixer(h, w_q, w_o, k_sh, v_sh, cos, sin):
    bsz, seq, _ = h.shape
    q = apply_partial_rope((h @ w_q).reshape(bsz, seq, N_Q_HEADS, HEAD_DIM), cos, sin)
    outs, lses = [], []
    for g, (win, dil) in enumerate(DILATED_GROUPS):
        sl = slice(g * HEADS_PER_GROUP, (g + 1) * HEADS_PER_GROUP)
        o, l = dilated_window_attention(q[:, :, sl], k_sh[:, :, sl], v_sh[:, :, sl], win, dil)
        outs.append(o)
        lses.append(l)
    o = jnp.stack(outs, axis=0).astype(jnp.float32)
    w = jax.nn.softmax(jnp.stack(lses, axis=0), axis=0)
    mixed = jnp.sum(w[..., None] * o, axis=0).astype(h.dtype)
    return mixed.reshape(bsz, seq, OUT_WIDTH) @ w_o


def shared_kv(x, g, shift, scale, w_kv, cos, sin):
    bsz, seq, _ = x.shape
    h = modulate(rms_norm(x, g), shift, scale)
    k, v = jnp.split(h @ w_kv, 2, axis=-1)
    k = apply_partial_rope(k.reshape(bsz, seq, N_Q_HEADS, HEAD_DIM), cos, sin)
    v = v.reshape(bsz, seq, N_Q_HEADS, HEAD_DIM)
    return k, v


def setup_inputs(seed: int = 0) -> dict:
    key = jax.random.key(seed)
    ks = jax.random.split(key, 24)
    f32 = jnp.float32
    D, F = D_MODEL, D_FF

    def nrm(k, shape, fan_in, mult=1.0):
        return jax.random.normal(k, shape, f32) * (mult * fan_in ** -0.5)

    x = jax.random.normal(ks[0], (BATCH, SEQ, D), f32)
    c = jax.random.normal(ks[1], (BATCH, D), f32)
    offset = jax.random.randint(ks[2], (BATCH, 1), 0, 1024, dtype=jnp.int32)
    positions = offset + jnp.arange(SEQ, dtype=jnp.int32)[None, :]
    return {
        "x": x,
        "c": c,
        "positions": positions,
        "norm_g": 1.0 + 0.02 * jax.random.normal(ks[3], (DEPTH, 3, D), f32),
        "ada_w": nrm(ks[4], (DEPTH, D, N_MOD * D), D, 0.1),
        "ada_b": 0.01 * jax.random.normal(ks[5], (DEPTH, N_MOD * D), f32),
        "ffn1_w_in": nrm(ks[6], (DEPTH, D, 2 * F), D),
        "ffn1_w_out": nrm(ks[7], (DEPTH, F, D), F),
        "ffn2_w_in": nrm(ks[8], (DEPTH, D, 2 * F), D),
        "ffn2_w_out": nrm(ks[9], (DEPTH, F, D), F),
        "conv_w_in": nrm(ks[10], (N_A_LAYERS, D, 3 * D), D),
        "conv_w": nrm(ks[11], (N_A_LAYERS, CONV_WIDTH, D), CONV_WIDTH),
        "conv_w_out": nrm(ks[12], (N_A_LAYERS, D, D), D),
        "kv_norm_g": 1.0 + 0.02 * jax.random.normal(ks[13], (D,), f32),
        "kv_ada_w": nrm(ks[14], (D, 2 * D), D, 0.1),
        "kv_ada_b": 0.01 * jax.random.normal(ks[15], (2 * D,), f32),
        "w_kv": nrm(ks[16], (D, 2 * Q_WIDTH), D),
        "attn_w_q": nrm(ks[17], (N_B_LAYERS, D, Q_WIDTH), D),
        "attn_w_o": nrm(ks[18], (N_B_LAYERS, OUT_WIDTH, D), OUT_WIDTH),
        "final_norm_g": 1.0 + 0.02 * jax.random.normal(ks[19], (D,), f32),
    }


def reference(x, c, positions, norm_g, ada_w, ada_b, ffn1_w_in, ffn1_w_out, ffn2_w_in, ffn2_w_out,
              conv_w_in, conv_w, conv_w_out, kv_norm_g, kv_ada_w, kv_ada_b, w_kv,
              attn_w_q, attn_w_o, final_norm_g):
    cond = jax.nn.silu(c)
    cos, sin = rope_tables(positions)
    k_sh = v_sh = None
    for layer in range(DEPTH):
        if layer == N_A_LAYERS:
            kv_shift, kv_scale = jnp.split(cond @ kv_ada_w + kv_ada_b, 2, axis=-1)
            k_sh, v_sh = shared_kv(x, kv_norm_g, kv_shift, kv_scale, w_kv, cos, sin)
        mods = cond @ ada_w[layer] + ada_b[layer]
        sh1, sc1, g1, sh2, sc2, g2, sh3, sc3, g3 = jnp.split(mods, N_MOD, axis=-1)
        h = modulate(rms_norm(x, norm_g[layer, 0]), sh1, sc1)
        x = x + FFN_RES_WEIGHT * (1 + g1)[:, None, :] * swiglu(h, ffn1_w_in[layer], ffn1_w_out[layer])
        h = modulate(rms_norm(x, norm_g[layer, 1]), sh2, sc2)
        if layer < N_A_LAYERS:
            mix = short_conv_mixer(h, conv_w_in[layer], conv_w[layer], conv_w_out[layer])
        else:
            j = layer - N_A_LAYERS
            mix = dilated_attention_mixer(h, attn_w_q[j], attn_w_o[j], k_sh, v_sh, cos, sin)
        x = x + (1 + g2)[:, None, :] * mix
        h = modulate(rms_norm(x, norm_g[layer, 2]), sh3, sc3)
        x = x + FFN_RES_WEIGHT * (1 + g3)[:, None, :] * swiglu(h, ffn2_w_in[layer], ffn2_w_out[layer])
    return rms_norm(x, final_norm_g)
```

```python
import numpy as np
import concourse.bass as bass
import concourse.mybir as mybir
from concourse.bass_utils import run_bass_kernel_spmd

F32 = mybir.dt.float32
BF16 = mybir.dt.bfloat16
I32 = mybir.dt.int32
AF = mybir.ActivationFunctionType
ALU = mybir.AluOpType
AX = mybir.AxisListType

D = 1024
SEQ = 8192
NB = 4
DFF = 2816
HALO = 2048 + 512
OWN = 4096
R = HALO + OWN
TILES = [(512 * i, 512) for i in range(13)]
OWN_T0 = 5
FPARTS = [(0, 8), (8, 7), (15, 7)]
EPS = 1e-5
NEG = -30000.0
TWO_PI = 6.283185307179586
PI = 3.141592653589793


class Grp:
    __slots__ = ("chan", "count")

    def __init__(self, chan):
        self.chan = chan
        self.count = 0


class Op:
    __slots__ = ("eng", "fn", "deps", "sig", "ticket", "grp", "isdma")

    def __init__(self, eng, fn):
        self.eng = eng
        self.fn = fn
        self.deps = []
        self.sig = False
        self.ticket = 0
        self.grp = None
        self.isdma = False


class Prog:
    ENGS = ("pe", "act", "dve", "pool", "sp")

    def __init__(self, nc):
        self.nc = nc
        self.eng_ops = {e: [] for e in self.ENGS}
        self.last_writer = {}
        self.readers = {}
        self.chan_cnt = {}
        self.last_dma = {}
        self.nbar = 0

    def _deps(self, o, reads, writes):
        deps = {}
        lw = self.last_writer
        rd = self.readers
        for b in reads:
            w = lw.get(b)
            if w is not None:
                deps[id(w)] = w
        for b in writes:
            w = lw.get(b)
            if w is not None:
                deps[id(w)] = w
            r = rd.get(b)
            if r:
                for x in r.values():
                    deps[id(x)] = x
        deps.pop(id(o), None)
        for b in writes:
            lw[b] = o
            rd[b] = {}
        key = ("dma", id(o)) if o.isdma else o.eng
        for b in reads:
            r = rd.get(b)
            if r is None:
                r = rd[b] = {}
            r[key] = o
        for d in deps.values():
            if d.eng == "pe" and o.eng == "pe" and not d.isdma and not o.isdma:
                continue
            d.sig = True
            o.deps.append(d)

    def op(self, eng, fn, reads=(), writes=()):
        o = Op(eng, fn)
        self._deps(o, reads, writes)
        self.eng_ops[eng].append(o)
        return o

    def dma(self, eng, fn, reads=(), writes=(), chan=None):
        o = Op(eng, fn)
        o.isdma = True
        grp = Grp(chan)
        o.grp = grp
        self._deps(o, reads, writes)
        c = self.chan_cnt.get(chan, 0) + 1
        self.chan_cnt[chan] = c
        grp.count = c
        self.last_dma[chan] = o
        self.eng_ops[eng].append(o)
        return o

    def barrier(self):
        self.nbar += 1
        firsts = []
        for e in self.ENGS:
            o = Op(e, lambda h: h.drain())
            o.sig = True
            self.eng_ops[e].append(o)
            firsts.append(o)
        dmas = list(self.last_dma.values())
        for e in self.ENGS:
            o = Op(e, lambda h: h.nop())
            for f in firsts:
                if f.eng != e:
                    o.deps.append(f)
            o.deps.extend(dmas)
            self.eng_ops[e].append(o)

    def emit(self):
        nc = self.nc
        sems = {e: nc.alloc_semaphore("s_" + e) for e in self.ENGS}
        csems = {c: nc.alloc_semaphore("c_" + str(c)) for c in self.chan_cnt}
        for e in self.ENGS:
            t = 0
            for o in self.eng_ops[e]:
                if o.isdma:
                    continue
                if o.sig:
                    t += 1
                    o.ticket = t

        def run(e, h):
            seen = {}
            for o in self.eng_ops[e]:
                for d in o.deps:
                    if d.isdma:
                        key = ("c", d.grp.chan)
                        val = d.grp.count * 16
                        s = csems[d.grp.chan]
                    else:
                        key = d.eng
                        val = d.ticket
                        s = sems[d.eng]
                    if seen.get(key, 0) < val:
                        h.wait_ge(s, val)
                        seen[key] = val
                ins = o.fn(h)
                if o.isdma:
                    ins.then_inc(csems[o.grp.chan], 16)
                elif o.sig:
                    ins.then_inc(sems[e], 1)

        with nc.Block() as block:
            @block.tensor
            def _(h):
                run("pe", h)

            @block.scalar
            def _(h):
                run("act", h)

            @block.vector
            def _(h):
                run("dve", h)

            @block.gpsimd
            def _(h):
                run("pool", h)

            @block.sync
            def _(h):
                run("sp", h)


class K:
    pass


def build(stop_after=None, dbg=False):
    nc = bass.Bass("TRN2", target_bir_lowering=False)
    P = Prog(nc)
    k = K()

    def din(name, shape, dt=F32):
        return nc.dram_tensor(name, list(shape), dt, kind="ExternalInput").ap()

    def dscr(name, shape, dt=F32):
        return nc.dram_tensor(name, list(shape), dt, kind=("ExternalOutput" if dbg else "Internal")).ap()

    xin = din("xin", [128, 8, R])
    cvec = din("cvec", [128, 8])
    posd = din("pos", [1, R], I32)
    flagd = din("flag", [128, 1])
    maskFd = din("maskF", [128, 256])
    maskNd = din("maskN", [128, 256])
    seld = din("sel", [128, 128])
    identd = din("ident", [128, 128])
    ropecd = din("ropec", [128, 2])
    adaw = din("adaw", [4, 1024, 9216])
    adab = din("adab", [128, 4, 72])
    kvadaw = din("kvadaw", [1024, 2048])
    kvadab = din("kvadab", [128, 16])
    normg = din("normg", [128, 96])
    kvnormg = din("kvnormg", [128, 8])
    fnormg = din("fnormg", [128, 8])
    f_win = [din("ffn1_w_in", [4, 1024, 5632]), din("ffn2_w_in", [4, 1024, 5632])]
    f_wout = [din("ffn1_w_out", [4, 2816, 1024]), din("ffn2_w_out", [4, 2816, 1024])]
    cw_in = din("conv_w_in", [2, 1024, 3072])
    cw_out = din("conv_w_out", [2, 1024, 1024])
    convwd = din("convw", [128, 48])
    wkv = din("w_kv", [1024, 3072])
    wksw = din("wk_sw", [1024, 1536])
    wq = din("attn_w_q", [2, 1024, 1536])
    wqsw = din("wq_sw", [2, 1024, 1536])
    wo = din("attn_w_o", [2, 512, 1024])
    outd = nc.dram_tensor("out", [128, 8, OWN], F32, kind="ExternalOutput").ap()

    X = dscr("Xs", [128, 8, R])
    H = dscr("Hs", [128, 8, R], BF16)
    KT = dscr("KTs", [128, 12, R], BF16)
    Vd = dscr("Vs", [12, R, 128], BF16)
    QT = dscr("QTs", [128, 12, OWN], BF16)
    MIX = dscr("MIXs", [128, 4, OWN], BF16)

    WB = 24576
    wb = [nc.alloc_sbuf_tensor("s_wb%d" % i, [128, WB], BF16) for i in range(2)]
    wsm = nc.alloc_sbuf_tensor("s_wsm", [128, 8192], BF16)
    wstg = nc.alloc_sbuf_tensor("s_wstg", [128, 2, 1024], F32)
    xs = nc.alloc_sbuf_tensor("s_xs", [128, 2, 8, 512], F32)
    ones = nc.alloc_sbuf_tensor("s_ones", [128, 128], F32)
    cond = nc.alloc_sbuf_tensor("s_cond", [128, 8], F32)
    mods = nc.alloc_sbuf_tensor("s_mods", [128, 4, 72], F32)
    kvm = nc.alloc_sbuf_tensor("s_kvm", [128, 16], F32)
    ng = nc.alloc_sbuf_tensor("s_ng", [128, 96], F32)
    kvng = nc.alloc_sbuf_tensor("s_kvng", [128, 8], F32)
    fng = nc.alloc_sbuf_tensor("s_fng", [128, 8], F32)
    gsb = nc.alloc_sbuf_tensor("s_gsb", [128, 14, 8], F32)
    gate = nc.alloc_sbuf_tensor("s_gate", [128, 12, 8], F32)
    cwt = nc.alloc_sbuf_tensor("s_cwt", [128, 48], F32)
    flag = nc.alloc_sbuf_tensor("s_flag", [128, 1], F32)
    ropec = nc.alloc_sbuf_tensor("s_ropec", [128, 2], F32)
    tmpb = nc.alloc_sbuf_tensor("s_tmpb", [128, 96], F32)
    epsb = nc.alloc_sbuf_tensor("s_epsb", [128, 1], F32)
    hs = nc.alloc_sbuf_tensor("s_hs", [128, 8, 512], BF16)
    gt = nc.alloc_sbuf_tensor("s_gt", [128, 8, 512], BF16)
    tm = nc.alloc_sbuf_tensor("s_tm", [128, 4, 512], F32)
    rstd = nc.alloc_sbuf_tensor("s_rstd", [128, 512], F32)
    vt = nc.alloc_sbuf_tensor("s_vt", [128, 2, 516], F32)
    vh = nc.alloc_sbuf_tensor("s_vh", [128, 8, 2], F32)
    ropeC = nc.alloc_sbuf_tensor("s_ropeC", [128, 512], F32)
    ropeS = nc.alloc_sbuf_tensor("s_ropeS", [128, 512], F32)
    posi = nc.alloc_sbuf_tensor("s_posi", [128, 512], I32)
    fout = nc.alloc_sbuf_tensor("s_fout", [128, 8, 512], F32)
    ktile = fout[:].bitcast(BF16)[:, :, :].rearrange("p a (b t) -> p (a b) t", t=512)[:, 0:12, :]
    ps = [nc.alloc_psum_tensor("p_ps%d" % i, [128, 512], F32) for i in range(7)]
    psb = nc.alloc_psum_tensor("p_psb", [128, 1024], BF16)
    print("sbuf remaining after alloc", nc.sbuf_bytes_remaining)

    state = {"piece_i": 0, "pending": [], "npass": 0}

    def load_piece(src, dst, dstname):
        i = state["piece_i"]
        state["piece_i"] = i + 1
        s = i % 2
        n = src.shape[-1]
        P.dma("act", lambda h: h.dma_start(out=wstg[:, s, 0:n], in_=src), writes=["wstg%d" % s], chan="wstg%d" % s)
        P.op("pool", lambda h: h.tensor_copy(dst, wstg[:, s, 0:n]), reads=["wstg%d" % s], writes=[dstname])

    def pump(npieces):
        for _ in range(npieces):
            if not state["pending"]:
                return
            load_piece(*state["pending"].pop(0))

    def rows(w2d, kk):
        return w2d[kk * 128:(kk + 1) * 128, :]

    def queue_weights(specs):
        for (blk, c0, ncol, dst, name) in specs:
            o = 0
            while o < ncol:
                n = min(1024, ncol - o)
                state["pending"].append((blk[:, c0 + o:c0 + o + n], dst[:, o:o + n], name))
                o += n

    def ld(dst, src, name):
        P.dma("sp", lambda h: h.dma_start(out=dst, in_=src), writes=[name], chan="c_" + name)

    ld(cond[:], cvec, "cond")
    ld(ng[:], normg, "ng")
    ld(kvng[:], kvnormg, "kvng")
    ld(fng[:], fnormg, "fng")
    ld(cwt[:], convwd, "cwt")
    ld(flag[:], flagd, "flag")
    ld(ropec[:], ropecd, "ropec")
    ld(mods[:], adab, "mods")
    ld(kvm[:], kvadab, "kvm")
    P.op("dve", lambda h: h.memset(ones[:], 1.0), writes=["ones"])
    P.op("dve", lambda h: h.memset(epsb[:], EPS), writes=["epsb"])
    P.op("act", lambda h: h.activation(out=cond[:], in_=cond[:], func=AF.Silu), reads=["cond"], writes=["cond"])
    wbf = [w[:].bitcast(F32) for w in wb]
    condrep = fout[:, 0:2, :].rearrange("p a t -> p (a t)")
    identf0 = vt[:, 0, 0:128]
    P.dma("sp", lambda h: h.dma_start(out=identf0, in_=identd), writes=["vt0"], chan="c_identf")
    for kk in range(8):
        P.op("dve", lambda h, kk=kk: h.tensor_scalar(out=condrep[:, kk * 128:(kk + 1) * 128], in0=ones[:], scalar1=cond[:, kk:kk + 1],
                                                     scalar2=None, op0=ALU.mult), reads=["ones", "cond"], writes=["condrep"])
    nblk = 0
    for l in range(5):
        ncol = 9216 if l < 4 else 2048
        src_all = (adaw[l] if l < 4 else kvadaw).rearrange("(k p) n -> p k n", p=128)
        for cb in range(ncol // 512):
            s_ = nblk % 2
            pst = ps[nblk % 2]
            pstn = "ps%d" % (nblk % 2)
            nblk += 1
            stg = wbf[s_][:, 0:4096].rearrange("p (k n) -> p k n", k=8)
            P.dma("sp", lambda h, stg=stg, src_all=src_all, cb=cb: h.dma_start(out=stg, in_=src_all[:, :, cb * 512:(cb + 1) * 512]),
                  writes=["wb%d" % s_], chan="wbl%d" % s_)
            for kk in range(8):
                P.op("pe", lambda h, stg=stg, kk=kk, pst=pst: h.matmul(pst[:, :], condrep[:, kk * 128:(kk + 1) * 128], stg[:, kk, :],
                                                                      start=(kk == 0), stop=(kk == 7)),
                     reads=["wb%d" % s_, "condrep"], writes=[pstn])
            for jj in range(4):
                j = cb * 4 + jj
                dstc = (mods[:, l, j:j + 1] if l < 4 else kvm[:, j:j + 1])
                dname = "mods" if l < 4 else "kvm"
                tcol = tmpb[:, (j % 8):(j % 8) + 1]
                P.op("dve", lambda h, pst=pst, jj=jj: h.tensor_tensor(out=tm[:, 0, 0:128], in0=pst[:, jj * 128:(jj + 1) * 128], in1=identf0,
                                                                      op=ALU.mult), reads=[pstn, "vt0"], writes=["tm0"])
                P.op("dve", lambda h, tcol=tcol: h.tensor_reduce(out=tcol, in_=tm[:, 0, 0:128], axis=AX.X, op=ALU.add),
                     reads=["tm0"], writes=["tmpb"])
                P.op("dve", lambda h, dstc=dstc, tcol=tcol: h.tensor_tensor(out=dstc, in0=dstc, in1=tcol, op=ALU.add),
                     reads=["tmpb", dname], writes=[dname])
    for l in range(4):
        for i in range(3):
            sc = mods[:, l, (3 * i + 1) * 8:(3 * i + 2) * 8]
            gg = mods[:, l, (3 * i + 2) * 8:(3 * i + 3) * 8]
            idx = l * 3 + i
            P.op("dve", lambda h, sc=sc, idx=idx: h.scalar_tensor_tensor(
                out=gsb[:, idx, :], in0=sc, scalar=1.0, in1=ng[:, idx * 8:(idx + 1) * 8], op0=ALU.add, op1=ALU.mult),
                reads=["mods", "ng"], writes=["gsb"])
            mul = 1.0 if i == 1 else 0.5
            P.op("dve", lambda h, gg=gg, idx=idx, mul=mul: h.tensor_scalar(
                out=gate[:, idx, :], in0=gg, scalar1=1.0, scalar2=mul, op0=ALU.add, op1=ALU.mult),
                reads=["mods"], writes=["gate"])
    P.op("dve", lambda h: h.scalar_tensor_tensor(out=gsb[:, 12, :], in0=kvm[:, 8:16], scalar=1.0, in1=kvng[:],
                                                 op0=ALU.add, op1=ALU.mult), reads=["kvm", "kvng"], writes=["gsb"])
    P.op("dve", lambda h: h.tensor_copy(gsb[:, 13, :], fng[:]), reads=["fng"], writes=["gsb"])
    P.barrier()

    def load_x(ti, slot, src=None):
        u0, n = TILES[ti]
        s_ap = (X if src is None else src)[:, :, u0:u0 + n]
        P.dma("sp", lambda h: h.dma_start(out=xs[:, slot, :, 0:n], in_=s_ap), reads=["X%d" % ti],
              writes=["xs%d" % slot], chan="xs%d" % slot)

    def store_x(ti, slot):
        u0, n = TILES[ti]
        P.dma("sp", lambda h: h.dma_start(out=X[:, :, u0:u0 + n], in_=xs[:, slot, :, 0:n]), reads=["xs%d" % slot],
              writes=["X%d" % ti], chan="xst%d" % slot)

    def norm_tile(slot, n, gidx, shift, out_fn):
        xn = "xs%d" % slot
        for c in range(8):
            t = c % 2
            P.op("act", lambda h, c=c, t=t: h.activation(out=tm[:, t, 0:n], in_=xs[:, slot, c, 0:n], func=AF.Square),
                 reads=[xn], writes=["tm%d" % t])
            P.op("pe", lambda h, c=c, t=t: h.matmul(ps[6][:, 0:n], ones[:], tm[:, t, 0:n], start=(c == 0), stop=(c == 7)),
                 reads=["ones", "tm%d" % t], writes=["ps6"])
        P.op("act", lambda h: h.activation(out=rstd[:, 0:n], in_=ps[6][:, 0:n], func=AF.Sqrt, bias=epsb[:, 0:1], scale=1.0 / D),
             reads=["ps6", "epsb"], writes=["rstd"])
        P.op("dve", lambda h: h.reciprocal(rstd[:, 0:n], rstd[:, 0:n]), reads=["rstd"], writes=["rstd"])
        for c in range(8):
            t = 2 + c % 2
            P.op("dve", lambda h, c=c, t=t: h.tensor_tensor(out=tm[:, t, 0:n], in0=xs[:, slot, c, 0:n], in1=rstd[:, 0:n],
                                                            op=ALU.mult), reads=[xn, "rstd"], writes=["tm%d" % t])
            dst, dname = out_fn(c)
            if shift is not None:
                P.op("act", lambda h, c=c, t=t, dst=dst: h.activation(
                    out=dst, in_=tm[:, t, 0:n], func=AF.Identity, bias=shift[:, c:c + 1], scale=gsb[:, gidx, c:c + 1]),
                    reads=["tm%d" % t, "gsb", "mods", "kvm"], writes=[dname])
            else:
                P.op("act", lambda h, c=c, t=t, dst=dst: h.activation(
                    out=dst, in_=tm[:, t, 0:n], func=AF.Identity, scale=gsb[:, gidx, c:c + 1]),
                    reads=["tm%d" % t, "gsb"], writes=[dname])

    def hs_out(n):
        return lambda c: (hs[:, c, 0:n], "hs")

    def store_h(ti, n):
        u0 = TILES[ti][0]
        P.dma("sp", lambda h: h.dma_start(out=H[:, :, u0:u0 + n], in_=hs[:, :, 0:n]), reads=["hs"], writes=["H%d" % ti],
              chan="hst")

    def load_h(ti, n):
        u0 = TILES[ti][0]
        P.dma("sp", lambda h: h.dma_start(out=hs[:, :, 0:n], in_=H[:, :, u0:u0 + n]), reads=["H%d" % ti], writes=["hs"],
              chan="hld")

    def proj_out(wout_v, nk, src_fn, slot, n, gidx):
        for m in range(8):
            pb = ps[4 + m % 2]
            pn = "ps%d" % (4 + m % 2)
            for kk in range(nk):
                sap, sname = src_fn(kk)
                P.op("pe", lambda h, m=m, kk=kk, pb=pb, sap=sap: h.matmul(
                    pb[:, 0:n], wout_v[:, kk, m * 128:(m + 1) * 128], sap, start=(kk == 0), stop=(kk == nk - 1)),
                    reads=[state["wname_out"], sname], writes=[pn])
            P.op("dve", lambda h, m=m, pb=pb: h.scalar_tensor_tensor(
                out=xs[:, slot, m, 0:n], in0=pb[:, 0:n], scalar=gate[:, gidx, m:m + 1], in1=xs[:, slot, m, 0:n],
                op0=ALU.mult, op1=ALU.add), reads=[pn, "gate", "xs%d" % slot], writes=["xs%d" % slot])

    def begin_pass(tiles, first_src=None):
        state["npass"] += 1
        state["tiles"] = tiles
        state["first_src"] = first_src

    def ffn_weights(l, f, j, buf):
        c0, ncj = FPARTS[j]
        specs = []
        win = f_win[f][l]
        wout = f_wout[f][l]
        w_in_v = wb[buf][:, 0:8 * 2 * ncj * 128].rearrange("p (k n) -> p k n", k=8)
        w_out_v = wb[buf][:, 16384:16384 + ncj * 1024].rearrange("p (k n) -> p k n", k=ncj)
        for kk in range(8):
            blk = rows(win, kk)
            specs.append((blk, c0 * 128, ncj * 128, w_in_v[:, kk, 0:ncj * 128], "wb%d" % buf))
            specs.append((blk, DFF + c0 * 128, ncj * 128, w_in_v[:, kk, ncj * 128:2 * ncj * 128], "wb%d" % buf))
        for kk in range(ncj):
            specs.append((rows(wout, c0 + kk), 0, 1024, w_out_v[:, kk, :], "wb%d" % buf))
        return specs, (w_in_v, w_out_v, ncj)

    def ffn_pass(l, f, j, buf, wv, tiles, src0=None):
        w_in_v, w_out_v, ncj = wv
        gidx = l * 3 + (0 if f == 0 else 2)
        shift = mods[:, l, (0 if f == 0 else 6) * 8:(0 if f == 0 else 6) * 8 + 8]
        wname = "wb%d" % buf
        state["wname_out"] = wname
        npump = (len(state["pending"]) + len(tiles) - 1) // max(1, len(tiles)) + 1
        load_x(tiles[0], 0, src0 if j == 0 else None)

        def body(ii, ti, n, slot):
            if ii + 1 < len(tiles):
                load_x(tiles[ii + 1], 1 - slot, src0 if j == 0 else None)
            pump(npump)
            if j == 0:
                norm_tile(slot, n, gidx, shift, hs_out(n))
                store_h(ti, n)
            else:
                load_h(ti, n)
            for cc in range(ncj):
                pa, pbb = ps[(cc % 2) * 2], ps[(cc % 2) * 2 + 1]
                na, nb_ = "ps%d" % ((cc % 2) * 2), "ps%d" % ((cc % 2) * 2 + 1)
                for (pp, pn, off) in ((pa, na, 0), (pbb, nb_, ncj * 128)):
                    for kk in range(8):
                        P.op("pe", lambda h, pp=pp, kk=kk, cc=cc, off=off: h.matmul(
                            pp[:, 0:n], w_in_v[:, kk, off + cc * 128:off + (cc + 1) * 128], hs[:, kk, 0:n],
                            start=(kk == 0), stop=(kk == 7)), reads=[wname, "hs"], writes=[pn])
                t = cc % 2
                P.op("act", lambda h, pa=pa, t=t: h.activation(out=tm[:, t, 0:n], in_=pa[:, 0:n], func=AF.Silu),
                     reads=[na], writes=["tm%d" % t])
                P.op("dve", lambda h, pbb=pbb, t=t, cc=cc: h.tensor_tensor(out=gt[:, cc, 0:n], in0=pbb[:, 0:n],
                                                                           in1=tm[:, t, 0:n], op=ALU.mult),
                     reads=[nb_, "tm%d" % t], writes=["gt%d" % cc])
            proj_out(w_out_v, ncj, lambda kk: (gt[:, kk, 0:n], "gt%d" % kk), slot, n, gidx)
            store_x(ti, slot)

        for ii, ti in enumerate(tiles):
            body(ii, ti, TILES[ti][1], ii % 2)
        pump(10 ** 6)
        P.barrier()

    def conv_weights(l, buf):
        w_in_v = wb[buf][:, 0:8 * 3072].rearrange("p (k n) -> p k n", k=8)
        w_out_v = wsm[:, 0:8192].rearrange("p (k n) -> p k n", k=8)
        specs = []
        for kk in range(8):
            specs.append((rows(cw_in[l], kk), 0, 3072, w_in_v[:, kk, :], "wb%d" % buf))
        for kk in range(8):
            specs.append((rows(cw_out[l], kk), 0, 1024, w_out_v[:, kk, :], "wsm"))
        return specs, (w_in_v, w_out_v)

    def conv_pass(l, buf, wv, tiles):
        w_in_v, w_out_v = wv
        gidx = l * 3 + 1
        shift = mods[:, l, 24:32]
        wname = "wb%d" % buf
        state["wname_out"] = "wsm"
        npump = (len(state["pending"]) + len(tiles) - 1) // max(1, len(tiles)) + 1
        P.op("dve", lambda h: h.memset(vh[:], 0.0), writes=["vh"])
        load_x(tiles[0], 0)

        def body(ii, ti, n, slot):
            if ii + 1 < len(tiles):
                load_x(tiles[ii + 1], 1 - slot)
            pump(npump)
            if ti == OWN_T0:
                P.op("dve", lambda h: h.tensor_scalar(out=vh[:], in0=vh[:], scalar1=flag[:, 0:1], scalar2=None, op0=ALU.mult),
                     reads=["vh", "flag"], writes=["vh"])
            norm_tile(slot, n, gidx, shift, hs_out(n))
            for c in range(8):
                s3 = (c % 2) * 3
                pB, pC, pU = ps[s3 % 6], ps[(s3 + 1) % 6], ps[(s3 + 2) % 6]
                nB, nC, nU = "ps%d" % (s3 % 6), "ps%d" % ((s3 + 1) % 6), "ps%d" % ((s3 + 2) % 6)
                for (pp, pn, off) in ((pB, nB, 0), (pC, nC, 1024), (pU, nU, 2048)):
                    for kk in range(8):
                        P.op("pe", lambda h, pp=pp, kk=kk, c=c, off=off: h.matmul(
                            pp[:, 0:n], w_in_v[:, kk, off + c * 128:off + (c + 1) * 128], hs[:, kk, 0:n],
                            start=(kk == 0), stop=(kk == 7)), reads=[wname, "hs"], writes=[pn])
                t = c % 2
                vn = "vt%d" % t
                P.op("act", lambda h, pC=pC, t=t: h.activation(out=tm[:, t, 0:n], in_=pC[:, 0:n], func=AF.Identity),
                     reads=[nC], writes=["tm%d" % t])
                P.op("dve", lambda h, c=c, t=t: h.tensor_copy(vt[:, t, 0:2], vh[:, c, :]), reads=["vh"], writes=[vn])
                P.op("dve", lambda h, pU=pU, t=t: h.tensor_tensor(out=vt[:, t, 2:2 + n], in0=pU[:, 0:n], in1=tm[:, t, 0:n],
                                                                  op=ALU.mult), reads=[nU, "tm%d" % t], writes=[vn])
                P.op("dve", lambda h, c=c, t=t: h.tensor_copy(vh[:, c, :], vt[:, t, n:n + 2]), reads=[vn], writes=["vh"])
                w0 = cwt[:, l * 24 + 0 * 8 + c:l * 24 + 0 * 8 + c + 1]
                w1 = cwt[:, l * 24 + 1 * 8 + c:l * 24 + 1 * 8 + c + 1]
                w2 = cwt[:, l * 24 + 2 * 8 + c:l * 24 + 2 * 8 + c + 1]
                t2 = 2 + t
                P.op("dve", lambda h, t=t, t2=t2, w2=w2: h.tensor_scalar(out=tm[:, t2, 0:n], in0=vt[:, t, 2:2 + n], scalar1=w2,
                                                                         scalar2=None, op0=ALU.mult),
                     reads=[vn, "cwt"], writes=["tm%d" % t2])
                P.op("dve", lambda h, t=t, t2=t2, w1=w1: h.scalar_tensor_tensor(
                    out=tm[:, t2, 0:n], in0=vt[:, t, 1:1 + n], scalar=w1, in1=tm[:, t2, 0:n], op0=ALU.mult, op1=ALU.add),
                    reads=[vn, "cwt", "tm%d" % t2], writes=["tm%d" % t2])
                P.op("dve", lambda h, t=t, t2=t2, w0=w0: h.scalar_tensor_tensor(
                    out=tm[:, t2, 0:n], in0=vt[:, t, 0:n], scalar=w0, in1=tm[:, t2, 0:n], op0=ALU.mult, op1=ALU.add),
                    reads=[vn, "cwt", "tm%d" % t2], writes=["tm%d" % t2])
                P.op("dve", lambda h, pB=pB, t2=t2, c=c: h.tensor_tensor(out=gt[:, c, 0:n], in0=pB[:, 0:n], in1=tm[:, t2, 0:n],
                                                                         op=ALU.mult),
                     reads=[nB, "tm%d" % t2], writes=["gt%d" % c])
            proj_out(w_out_v, 8, lambda kk: (gt[:, kk, 0:n], "gt%d" % kk), slot, n, gidx)
            store_x(ti, slot)

        for ii, ti in enumerate(tiles):
            body(ii, ti, TILES[ti][1], ii % 2)
        pump(10 ** 6)
        P.barrier()

    def rope_tile(u0, n):
        P.dma("sp", lambda h: h.dma_start(out=posi[:, 0:n], in_=posd[0:1, u0:u0 + n].partition_broadcast(128)),
              writes=["posi"], chan="posi")
        P.op("dve", lambda h: h.tensor_copy(tm[:, 0, 0:n], posi[:, 0:n]), reads=["posi"], writes=["tm0"])
        P.op("dve", lambda h: h.tensor_scalar(out=tm[:, 0, 0:n], in0=tm[:, 0, 0:n], scalar1=ropec[:, 0:1], scalar2=1.0 / TWO_PI,
                                              op0=ALU.mult, op1=ALU.mult), reads=["tm0", "ropec"], writes=["tm0"])
        for dst, off in ((1, 0.0), (2, 0.25)):
            dn = "tm%d" % dst
            P.op("dve", lambda h, dst=dst, off=off: h.tensor_scalar(out=tm[:, dst, 0:n], in0=tm[:, 0, 0:n], scalar1=off, scalar2=None,
                                                                    op0=ALU.add), reads=["tm0"], writes=[dn])
            P.op("dve", lambda h, dst=dst: h.tensor_copy(posi[:, 0:n], tm[:, dst, 0:n]), reads=[dn], writes=["posi"])
            P.op("dve", lambda h: h.tensor_copy(tm[:, 3, 0:n], posi[:, 0:n]), reads=["posi"], writes=["tm3"])
            P.op("dve", lambda h, dst=dst: h.tensor_tensor(out=tm[:, dst, 0:n], in0=tm[:, dst, 0:n], in1=tm[:, 3, 0:n], op=ALU.subtract),
                 reads=[dn, "tm3"], writes=[dn])
            P.op("dve", lambda h, dst=dst: h.tensor_scalar(out=tm[:, 3, 0:n], in0=tm[:, dst, 0:n], scalar1=0.5, scalar2=None,
                                                           op0=ALU.is_gt), reads=[dn], writes=["tm3"])
            P.op("dve", lambda h, dst=dst: h.tensor_tensor(out=tm[:, dst, 0:n], in0=tm[:, dst, 0:n], in1=tm[:, 3, 0:n], op=ALU.subtract),
                 reads=[dn, "tm3"], writes=[dn])
        SC = TWO_PI * (1.0 - 1e-6)
        P.op("act", lambda h: h.activation(out=ropeS[:, 0:n], in_=tm[:, 1, 0:n], func=AF.Sin, scale=SC), reads=["tm1"], writes=["ropeS"])
        P.op("act", lambda h: h.activation(out=ropeC[:, 0:n], in_=tm[:, 2, 0:n], func=AF.Sin, scale=SC), reads=["tm2"], writes=["ropeC"])
        P.op("dve", lambda h: h.tensor_scalar(out=ropeS[:, 0:n], in0=ropeS[:, 0:n], scalar1=ropec[:, 1:2], scalar2=None,
                                              op0=ALU.mult), reads=["ropeS", "ropec"], writes=["ropeS"])

    def qk_weights(wmat, wsw, buf):
        w1 = wb[buf][:, 0:8 * 1536].rearrange("p (k n) -> p k n", k=8)
        w2 = wb[buf][:, 8 * 1536:16 * 1536].rearrange("p (k n) -> p k n", k=8)
        specs = []
        for kk in range(8):
            specs.append((rows(wmat, kk), 0, 1536, w1[:, kk, :], "wb%d" % buf))
            specs.append((rows(wsw, kk), 0, 1536, w2[:, kk, :], "wb%d" % buf))
        return specs, (w1, w2)

    def qk_pass(buf, wv, tiles, gidx, shift, dst, dst_off, dname, save_h):
        w1, w2 = wv
        wname = "wb%d" % buf
        npump = (len(state["pending"]) + len(tiles) - 1) // max(1, len(tiles)) + 1
        load_x(tiles[0], 0)

        def body(ii, ti, u0, n, slot):
            if ii + 1 < len(tiles):
                load_x(tiles[ii + 1], 1 - slot)
            pump(npump)
            rope_tile(u0, n)
            norm_tile(slot, n, gidx, shift, hs_out(n))
            if save_h:
                store_h(ti, n)
            for c in range(12):
                pa, pbb = ps[(c % 2) * 2], ps[(c % 2) * 2 + 1]
                na, nb_ = "ps%d" % ((c % 2) * 2), "ps%d" % ((c % 2) * 2 + 1)
                for (pp, pn, wv_) in ((pa, na, w1), (pbb, nb_, w2)):
                    for kk in range(8):
                        P.op("pe", lambda h, pp=pp, kk=kk, c=c, wv_=wv_: h.matmul(
                            pp[:, 0:n], wv_[:, kk, c * 128:(c + 1) * 128], hs[:, kk, 0:n],
                            start=(kk == 0), stop=(kk == 7)), reads=[wname, "hs"], writes=[pn])
                t = c % 2
                P.op("dve", lambda h, pa=pa, t=t: h.tensor_tensor(out=tm[:, t, 0:n], in0=pa[:, 0:n], in1=ropeC[:, 0:n], op=ALU.mult),
                     reads=[na, "ropeC"], writes=["tm%d" % t])
                P.op("dve", lambda h, pbb=pbb, t=t: h.tensor_tensor(out=tm[:, 2 + t, 0:n], in0=pbb[:, 0:n], in1=ropeS[:, 0:n],
                                                                    op=ALU.mult), reads=[nb_, "ropeS"], writes=["tm%d" % (2 + t)])
                P.op("dve", lambda h, t=t, c=c: h.tensor_tensor(out=ktile[:, c, 0:n], in0=tm[:, t, 0:n], in1=tm[:, 2 + t, 0:n],
                                                                op=ALU.add), reads=["tm%d" % t, "tm%d" % (2 + t)], writes=["ktile"])
            o0 = u0 - dst_off
            P.dma("sp", lambda h, o0=o0, n=n: h.dma_start(out=dst[:, :, o0:o0 + n], in_=ktile[:, :, 0:n]), reads=["ktile"],
                  writes=["%s%d" % (dname, ti)], chan="ktst")

        for ii, ti in enumerate(tiles):
            body(ii, ti, TILES[ti][0], TILES[ti][1], ii % 2)
        pump(10 ** 6)
        P.barrier()

    vtile = ktile
    vtv = ktile[:, 0:3, :]

    def v_weights(buf):
        w1 = wb[buf][:, 0:8 * 1536].rearrange("p (k n) -> p k n", k=8)
        specs = []
        for kk in range(8):
            specs.append((rows(wkv, kk), 1536, 1536, w1[:, kk, :], "wb%d" % buf))
        return specs, (w1,)

    def v_pass(buf, wv, tiles):
        (w1,) = wv
        wname = "wb%d" % buf
        npump = (len(state["pending"]) + len(tiles) - 1) // max(1, len(tiles)) + 1
        def body(ti, u0, n):
            pump(npump)
            load_h(ti, n)
            for s in range(n // 128):
                for cb in range(3):
                    pp = ps[cb % 2]
                    pn = "ps%d" % (cb % 2)
                    for kk in range(8):
                        P.op("pe", lambda h, pp=pp, kk=kk, s=s, cb=cb: h.matmul(
                            pp[:, :], hs[:, kk, s * 128:(s + 1) * 128], w1[:, kk, cb * 512:(cb + 1) * 512],
                            start=(kk == 0), stop=(kk == 7)), reads=[wname, "hs"], writes=[pn])
                    if cb % 2 == 0:
                        P.op("act", lambda h, pp=pp, cb=cb: h.activation(out=vtv[:, cb, :], in_=pp[:, :], func=AF.Identity),
                             reads=[pn], writes=["ktile"])
                    else:
                        P.op("dve", lambda h, pp=pp, cb=cb: h.tensor_copy(vtv[:, cb, :], pp[:, :]), reads=[pn], writes=["ktile"])
                t0 = u0 + s * 128
                P.dma("sp", lambda h, t0=t0: h.dma_start(
                    out=Vd[:, t0:t0 + 128, :].rearrange("c t f -> t c f"),
                    in_=vtv.rearrange("p a (b f) -> p (a b) f", f=128)), reads=["ktile"], writes=["V%d" % ti], chan="vst")

        for ti in tiles:
            body(ti, TILES[ti][0], TILES[ti][1])
        pump(10 ** 6)
        P.barrier()

    def att_pass(buf):
        KTw = wb[buf][:, 0:4096]
        Qg = wb[buf][:, 4096:6144]
        Vb = wb[buf][:, 6144:6144 + 32 * 128].rearrange("p (b f) -> p b f", f=128)
        Og = wb[buf][:, 10240:10240 + 3 * 2048].rearrange("p (g t) -> p g t", g=3)
        Mb0 = wb[buf][:, 16384:16384 + 4096].bitcast(F32)
        Db0 = wb[buf][:, 20480:20480 + 4096].bitcast(F32)
        ob = wb[1 - buf]

        def rowbuf(i):
            return ob[:, i * 4096:(i + 1) * 4096].bitcast(F32)
        MM = [Mb0, rowbuf(0), rowbuf(1)]
        DD = [Db0, rowbuf(2), rowbuf(3)]
        MxB = rowbuf(4)
        ZB = rowbuf(5)
        mixb = gt[:, 0:4, :].rearrange("p a t -> p (a t)")
        Sm = tm[:, 0:2, 0:256]
        Pt = hs[:, 0:2, 0:256]
        PT = hs[:, 2:4, 0:256]
        ST = tmpb
        maskN = tm[:, 2, 0:256]
        maskF = tm[:, 3, 0:256]
        sel = rstd[:, 0:128]
        ident = hs[:, 4, 0:128]
        identf = vt[:, 0, 0:128]
        rowt = ropeC
        rowt2 = ropeS
        wn = "wb%d" % buf
        P.dma("sp", lambda h: h.dma_start(out=maskN, in_=maskNd), writes=["tm2"], chan="mk")
        P.dma("sp", lambda h: h.dma_start(out=maskF, in_=maskFd), writes=["tm3"], chan="mk2")
        P.dma("sp", lambda h: h.dma_start(out=sel, in_=seld), writes=["rstd"], chan="mk3")
        P.dma("sp", lambda h: h.dma_start(out=identf, in_=identd), writes=["vt0"], chan="mk4")
        P.op("dve", lambda h: h.tensor_copy(ident, identf), reads=["vt0"], writes=["ident"])
        ps6b = ps[6][:].bitcast(BF16)

        def att_block(g, d, s, r, span, first):
            mk, mkn = (maskF, "tm3") if first else (maskN, "tm2")
            q0 = s * span + r
            qsl = slice(q0, q0 + 127 * d + 1, d)
            ksl = slice(q0, q0 + 255 * d + 1, d)
            bp, bc = s * d + r, (s + 1) * d + r
            stages = [[], [], [], []]
            for hh in range(2):
                hp = slice(hh * 64, hh * 64 + 64)
                pS = ps[hh]
                pSn = "ps%d" % hh
                pT = (psb if hh == 0 else ps6b)[:, 0:256]
                pTn = "psb" if hh == 0 else "ps6"

                def stA(hh=hh, hp=hp, pS=pS, pSn=pSn):
                    P.op("pe", lambda h: h.matmul(pS[:, 0:256], Qg[hp, qsl], KTw[hp, ksl], start=True, stop=True),
                         reads=["Qg", "KTw"], writes=[pSn])

                def stB(hh=hh, pS=pS, pSn=pSn):
                    P.op("dve", lambda h: h.tensor_tensor(out=Sm[:, hh, :], in0=pS[:, 0:256], in1=mk, op=ALU.add),
                         reads=[pSn, mkn], writes=["Sm%d" % hh])
                    P.op("dve", lambda h: h.tensor_reduce(out=ST[:, hh:hh + 1], in_=Sm[:, hh, :], axis=AX.X, op=ALU.max),
                         reads=["Sm%d" % hh], writes=["STm%d" % hh])
                    P.op("dve", lambda h: h.tensor_scalar(out=ST[:, 4 + hh:5 + hh], in0=ST[:, hh:hh + 1], scalar1=-0.125,
                                                          scalar2=None, op0=ALU.mult),
                         reads=["STm%d" % hh], writes=["STn%d" % hh])
                    P.op("act", lambda h: h.activation(out=Pt[:, hh, :], in_=Sm[:, hh, :], func=AF.Exp,
                                                       bias=ST[:, 4 + hh:5 + hh], scale=0.125, accum_out=ST[:, 2 + hh:3 + hh]),
                         reads=["Sm%d" % hh, "STn%d" % hh], writes=["Pt%d" % hh, "STd%d" % hh])

                def stC(hh=hh, pT=pT, pTn=pTn):
                    for kc in range(2):
                        P.op("pe", lambda h, kc=kc: h.transpose(pT[:, kc * 128:(kc + 1) * 128], Pt[:, hh, kc * 128:(kc + 1) * 128], ident),
                             reads=["Pt%d" % hh, "ident"], writes=[pTn])
                    P.op("act", lambda h: h.activation(out=PT[:, hh, :], in_=pT, func=AF.Identity), reads=[pTn], writes=["PT%d" % hh])

                def stD(hh=hh):
                    for kc, bb in ((0, bp), (1, bc)):
                        P.op("pe", lambda h, kc=kc, bb=bb: h.matmul(
                            ps[2][hh * 64:hh * 64 + 64, 0:128], Vb[:, bb, hh * 64:hh * 64 + 64], PT[:, hh, kc * 128:(kc + 1) * 128],
                            start=(kc == 0), stop=(kc == 1)), reads=["Vb", "PT%d" % hh], writes=["ps2_%d" % hh])

                stages[0].append(stA)
                stages[1].append(stB)
                stages[2].append(stC)
                stages[3].append(stD)
            for st in stages:
                for f in st:
                    f()
            P.op("dve", lambda h: h.tensor_copy(Og[:, g, qsl], ps[2][:, 0:128]), reads=["ps2_0", "ps2_1"], writes=["Og"])
            P.op("pe", lambda h: h.transpose(ps[3][0:2, 0:128], ST[:, 0:2], identf), reads=["STm0", "STm1", "vt0"], writes=["ps3"])
            P.op("pe", lambda h: h.transpose(ps[3][0:2, 128:256], ST[:, 2:4], identf), reads=["STd0", "STd1", "vt0"], writes=["ps3"])
            P.op("act", lambda h: h.activation(out=MM[g][0:2, qsl], in_=ps[3][0:2, 0:128], func=AF.Identity),
                 reads=["ps3"], writes=["Mb"])
            P.op("act", lambda h: h.activation(out=DD[g][0:2, qsl], in_=ps[3][0:2, 128:256], func=AF.Identity),
                 reads=["ps3"], writes=["Db"])

        cnt = 0
        for ss in range(2):
            o0 = ss * 2048
            for c in range(4):
                for g, d in ((0, 1), (1, 4), (2, 16)):
                    span = 128 * d
                    nspan = 2048 // span
                    ws = HALO + o0 - span
                    wl = span + 2048
                    ch = 4 * g + c
                    ktiles = sorted(set(u // 512 for u in (ws, ws + wl - 1)))
                    ktl = list(range(ktiles[0], ktiles[-1] + 1))
                    P.dma("sp", lambda h, ch=ch, ws=ws, wl=wl: h.dma_start(out=KTw[:, 0:wl], in_=KT[:, ch, ws:ws + wl]),
                          reads=["K%d" % t for t in ktl], writes=["KTw"], chan="ktw")
                    P.dma("sp", lambda h, ch=ch, o0=o0: h.dma_start(out=Qg[:, :], in_=QT[:, ch, o0:o0 + 2048]),
                          reads=["Q%d" % t for t in range(OWN_T0, 13)], writes=["Qg"], chan="qg")
                    nblk = (nspan + 1) * d
                    for sp_ in range(nspan + 1):
                        for rh in range(0, d, 8):
                            rn = min(8, d - rh)
                            src = Vd[ch, ws + sp_ * span:ws + (sp_ + 1) * span, :].rearrange("(k r) f -> k r f", r=d)[:, rh:rh + rn, :]
                            b0 = sp_ * d + rh
                            P.dma("sp", lambda h, src=src, b0=b0, rn=rn: h.dma_start(out=Vb[:, b0:b0 + rn, :], in_=src),
                                  reads=["V%d" % t for t in ktl], writes=["Vb"], chan="vb%d" % (cnt % 4))
                            cnt += 1
                    for s in range(nspan):
                        for r in range(d):
                            att_block(g, d, s, r, span, (ss == 0 and s == 0))
                M0, M1, M2 = MM[0][0:2, :], MM[1][0:2, :], MM[2][0:2, :]
                D0, D1, D2 = DD[0][0:2, :], DD[1][0:2, :], DD[2][0:2, :]
                Mx = MxB[0:2, :]
                Z = ZB[0:2, :]
                P.op("dve", lambda h: h.tensor_tensor(out=Mx, in0=M0, in1=M1, op=ALU.max), reads=["Mb"], writes=["Mx"])
                P.op("dve", lambda h: h.tensor_tensor(out=Mx, in0=Mx, in1=M2, op=ALU.max), reads=["Mb", "Mx"], writes=["Mx"])
                for gi, (Mg, Dg) in enumerate(((M0, D0), (M1, D1), (M2, D2))):
                    P.op("dve", lambda h, Mg=Mg: h.tensor_tensor(out=Mg, in0=Mg, in1=Mx, op=ALU.subtract), reads=["Mb", "Mx"], writes=["Mb"])
                    P.op("act", lambda h, Mg=Mg: h.activation(out=Mg, in_=Mg, func=AF.Exp, scale=0.125), reads=["Mb"], writes=["Mb"])
                    P.op("dve", lambda h, Mg=Mg, Dg=Dg: h.tensor_tensor(out=Dg, in0=Dg, in1=Mg, op=ALU.mult), reads=["Mb", "Db"], writes=["Db"])
                P.op("dve", lambda h: h.tensor_tensor(out=Z, in0=D0, in1=D1, op=ALU.add), reads=["Db"], writes=["Z"])
                P.op("dve", lambda h: h.tensor_tensor(out=Z, in0=Z, in1=D2, op=ALU.add), reads=["Db", "Z"], writes=["Z"])
                P.op("dve", lambda h: h.reciprocal(Z, Z), reads=["Z"], writes=["Z"])
                for Mg in (M0, M1, M2):
                    P.op("dve", lambda h, Mg=Mg: h.tensor_tensor(out=Mg, in0=Mg, in1=Z, op=ALU.mult), reads=["Mb", "Z"], writes=["Mb"])
                for tt in range(4):
                    tsl = slice(tt * 512, (tt + 1) * 512)
                    for gi in range(3):
                        pF = ps[4 + gi % 2]
                        pFn = "ps%d" % (4 + gi % 2)
                        P.op("pe", lambda h, gi=gi, tsl=tsl, pF=pF: h.matmul(pF[:, :], sel[0:2, :], MM[gi][0:2, tsl],
                                                                             start=True, stop=True), reads=["Mb", "rstd"], writes=[pFn])
                        if gi == 0:
                            P.op("dve", lambda h, tsl=tsl, pF=pF: h.tensor_tensor(out=rowt[:, :], in0=pF[:, :], in1=Og[:, 0, tsl], op=ALU.mult),
                                 reads=[pFn, "Og"], writes=["rowt"])
                        else:
                            P.op("dve", lambda h, gi=gi, tsl=tsl, pF=pF: h.tensor_tensor(out=rowt2[:, :], in0=pF[:, :], in1=Og[:, gi, tsl],
                                                                                        op=ALU.mult), reads=[pFn, "Og"], writes=["rowt2"])
                            P.op("dve", lambda h: h.tensor_tensor(out=rowt[:, :], in0=rowt[:, :], in1=rowt2[:, :], op=ALU.add),
                                 reads=["rowt", "rowt2"], writes=["rowt"])
                    P.op("act", lambda h, tsl=tsl: h.activation(out=mixb[:, tsl], in_=rowt[:, :], func=AF.Identity), reads=["rowt"],
                         writes=["mixb"])
                P.dma("sp", lambda h, c=c, o0=o0: h.dma_start(out=MIX[:, c, o0:o0 + 2048], in_=mixb), reads=["mixb"],
                      writes=["MIX"], chan="mixst")
        P.barrier()

    def o_weights(j):
        w_out_v = wsm[:, 0:4096].rearrange("p (k n) -> p k n", k=4)
        specs = []
        for kk in range(4):
            specs.append((rows(wo[j], kk), 0, 1024, w_out_v[:, kk, :], "wsm"))
        return specs, (w_out_v,)

    def o_pass(l, wv, tiles):
        (w_out_v,) = wv
        gidx = l * 3 + 1
        state["wname_out"] = "wsm"
        npump = (len(state["pending"]) + len(tiles) - 1) // max(1, len(tiles)) + 1
        load_x(tiles[0], 0)

        def body(ii, ti, u0, n, slot):
            if ii + 1 < len(tiles):
                load_x(tiles[ii + 1], 1 - slot)
            pump(npump)
            o0 = u0 - HALO
            P.dma("sp", lambda h, o0=o0, n=n: h.dma_start(out=gt[:, 0:4, 0:n], in_=MIX[:, :, o0:o0 + n]), reads=["MIX"],
                  writes=["gt0", "gt1", "gt2", "gt3"], chan="mixld")
            proj_out(w_out_v, 4, lambda kk: (gt[:, kk, 0:n], "gt%d" % kk), slot, n, gidx)
            store_x(ti, slot)

        for ii, ti in enumerate(tiles):
            body(ii, ti, TILES[ti][0], TILES[ti][1], ii % 2)
        pump(10 ** 6)
        P.barrier()

    def fin_pass(tiles):
        load_x(tiles[0], 0)

        def body(ii, ti, u0, n, slot):
            if ii + 1 < len(tiles):
                load_x(tiles[ii + 1], 1 - slot)
            norm_tile(slot, n, 13, None, lambda c: (fout[:, c, 0:n], "fout"))
            o0 = u0 - HALO
            P.dma("sp", lambda h, o0=o0, n=n: h.dma_start(out=outd[:, :, o0:o0 + n], in_=fout[:, :, 0:n]), reads=["fout"],
                  writes=["OUT%d" % ti], chan="outst")

        for ii, ti in enumerate(tiles):
            body(ii, ti, TILES[ti][0], TILES[ti][1], ii % 2)
        P.barrier()

    print("sbuf remaining final", nc.sbuf_bytes_remaining)

    ALLT = list(range(13))
    KVT = list(range(1, 13))
    OWNT = list(range(OWN_T0, 13))
    sched = []
    for l in range(2):
        for f in range(2):
            if f == 1:
                sched.append(("conv", l))
            for j in range(3):
                sched.append(("ffn", l, f, j))
    sched.append(("k",))
    sched.append(("v",))
    for l in range(2, 4):
        for j in range(3):
            sched.append(("ffn", l, 0, j))
        sched.append(("q", l))
        sched.append(("att", l))
        sched.append(("o", l))
        for j in range(3):
            sched.append(("ffn", l, 1, j))
    sched.append(("fin",))
    if stop_after is not None:
        sched = sched[:stop_after]

    def weights_for(item, buf):
        kind = item[0]
        if kind == "ffn":
            return ffn_weights(item[1], item[2], item[3], buf)
        if kind == "conv":
            return conv_weights(item[1], buf)
        if kind == "k":
            return qk_weights(wkv, wksw, buf)
        if kind == "v":
            return v_weights(buf)
        if kind == "q":
            return qk_weights(wq[item[1] - 2], wqsw[item[1] - 2], buf)
        if kind == "o":
            return o_weights(item[1] - 2)
        return [], None

    wviews = [None] * len(sched)
    if sched:
        specs, wviews[0] = weights_for(sched[0], 0)
        queue_weights(specs)
        pump(10 ** 6)
    first_x = True
    for pi, item in enumerate(sched):
        buf = pi % 2
        if pi + 1 < len(sched):
            specs, wviews[pi + 1] = weights_for(sched[pi + 1], (pi + 1) % 2)
            queue_weights(specs)
        kind = item[0]
        wv = wviews[pi]
        if kind == "ffn":
            l, f, j = item[1], item[2], item[3]
            tiles = ALLT if l < 2 else OWNT
            ffn_pass(l, f, j, buf, wv, tiles, src0=(xin if first_x else None))
            first_x = False
        elif kind == "conv":
            conv_pass(item[1], buf, wv, ALLT)
        elif kind == "k":
            qk_pass(buf, wv, KVT, 12, kvm[:, 0:8], KT, 0, "K", True)
        elif kind == "v":
            v_pass(buf, wv, KVT)
        elif kind == "q":
            l = item[1]
            qk_pass(buf, wv, OWNT, l * 3 + 1, mods[:, l, 24:32], QT, HALO, "Q", False)
        elif kind == "att":
            att_pass(buf)
        elif kind == "o":
            o_pass(item[1], wv, OWNT)
        elif kind == "fin":
            fin_pass(OWNT)
    P.barrier()
    P.emit()
    return nc


def host_inputs(x, c, positions, norm_g, ada_w, ada_b, ffn1_w_in, ffn1_w_out, ffn2_w_in, ffn2_w_out,
                conv_w_in, conv_w, conv_w_out, kv_norm_g, kv_ada_w, kv_ada_b, w_kv, attn_w_q, attn_w_o, final_norm_g):
    f32 = np.float32

    def fm(v):
        v = np.asarray(v, f32)
        return v.reshape(v.shape[:-1] + (8, 128))

    perm = np.arange(1536).reshape(24, 64)
    p2 = perm.copy()
    p2[:, 0:8] = perm[:, 8:16]
    p2[:, 8:16] = perm[:, 0:8]
    p2 = p2.reshape(-1)
    w_kv = np.asarray(w_kv, f32)
    attn_w_q = np.asarray(attn_w_q, f32)
    shared = {
        "adaw": np.ascontiguousarray(np.asarray(ada_w, f32)),
        "adab": np.ascontiguousarray(np.asarray(ada_b, f32).reshape(4, 72, 128).transpose(2, 0, 1)),
        "kvadaw": np.ascontiguousarray(np.asarray(kv_ada_w, f32)),
        "kvadab": np.ascontiguousarray(np.asarray(kv_ada_b, f32).reshape(16, 128).T),
        "normg": np.ascontiguousarray(np.asarray(norm_g, f32).reshape(4, 3, 8, 128).transpose(3, 0, 1, 2).reshape(128, 96)),
        "kvnormg": np.ascontiguousarray(np.asarray(kv_norm_g, f32).reshape(8, 128).T),
        "fnormg": np.ascontiguousarray(np.asarray(final_norm_g, f32).reshape(8, 128).T),
        "ffn1_w_in": np.ascontiguousarray(np.asarray(ffn1_w_in, f32)),
        "ffn2_w_in": np.ascontiguousarray(np.asarray(ffn2_w_in, f32)),
        "ffn1_w_out": np.ascontiguousarray(np.asarray(ffn1_w_out, f32)),
        "ffn2_w_out": np.ascontiguousarray(np.asarray(ffn2_w_out, f32)),
        "conv_w_in": np.ascontiguousarray(np.asarray(conv_w_in, f32)),
        "conv_w_out": np.ascontiguousarray(np.asarray(conv_w_out, f32)),
        "convw": np.ascontiguousarray(np.asarray(conv_w, f32).reshape(2, 3, 8, 128).transpose(3, 0, 1, 2).reshape(128, 48)),
        "w_kv": np.ascontiguousarray(w_kv),
        "wk_sw": np.ascontiguousarray(w_kv[:, 0:1536][:, p2]),
        "attn_w_q": np.ascontiguousarray(attn_w_q),
        "wq_sw": np.ascontiguousarray(attn_w_q[:, :, p2]),
        "attn_w_o": np.ascontiguousarray(np.asarray(attn_w_o, f32)),
    }
    qi = np.arange(128)[:, None]
    kj = np.arange(256)[None, :]
    validN = (kj >= qi) & (kj <= qi + 128)
    validF = validN & (kj >= 128)
    maskN = np.where(validN, 0.0, NEG).astype(f32)
    maskFm = np.where(validF, 0.0, NEG).astype(f32)
    sel = np.zeros((128, 128), f32)
    for b in (0, 32, 64):
        sel[b, 0:64] = 1.0
        sel[b + 1, 64:128] = 1.0
    ident = np.eye(128, dtype=f32)
    ropec = np.zeros((128, 2), f32)
    inv = (500000.0 ** (-np.arange(0, 16, 2, dtype=np.float32) / 16)).astype(f32)
    for p in range(128):
        i = p % 64
        if i < 8:
            ropec[p, 0] = inv[i]
            ropec[p, 1] = -1.0
        elif i < 16:
            ropec[p, 0] = inv[i - 8]
            ropec[p, 1] = 1.0
    shared.update({"maskN": maskN, "sel": sel, "ident": ident, "ropec": ropec})
    x = np.asarray(x, f32)
    c = np.asarray(c, f32)
    positions = np.asarray(positions, np.int32)
    in_maps = []
    for core in range(8):
        b, half = core // 2, core % 2
        s0 = half * OWN
        t0 = s0 - HALO
        xr = np.zeros((R, D), f32)
        pr = np.zeros((1, R), np.int32)
        lo = max(t0, 0)
        xr[lo - t0:] = x[b, lo:s0 + OWN]
        pr[0, lo - t0:] = positions[b, lo:s0 + OWN]
        m = dict(shared)
        m["xin"] = np.ascontiguousarray(xr.T.reshape(8, 128, R).transpose(1, 0, 2))
        m["cvec"] = np.ascontiguousarray(c[b].reshape(8, 128).T)
        m["pos"] = pr
        m["flag"] = np.full((128, 1), float(half), f32)
        m["maskF"] = maskFm if half == 0 else maskN
        in_maps.append(m)
    return in_maps


_NC_CACHE = {}


def kernel(**inputs):
    in_maps = host_inputs(**inputs)
    if "nc" not in _NC_CACHE:
        _NC_CACHE["nc"] = build()
    nc = _NC_CACHE["nc"]
    res = run_bass_kernel_spmd(nc, in_maps, core_ids=list(range(8)))
    out = np.zeros((NB, SEQ, D), np.float32)
    for core in range(8):
        b, half = core // 2, core % 2
        o = res.results[core]["out"]
        out[b, half * OWN:(half + 1) * OWN, :] = o.transpose(2, 1, 0).reshape(OWN, D)
    return out
```

```python
import numpy as np
import concourse.bass as bass
import concourse.mybir as mybir
from concourse.bass_utils import run_bass_kernel_spmd

F32 = mybir.dt.float32
BF16 = mybir.dt.bfloat16
I32 = mybir.dt.int32
AF = mybir.ActivationFunctionType
ALU = mybir.AluOpType
AX = mybir.AxisListType

D = 1024
SEQ = 8192
NB = 4
DFF = 2816
HALO = 2048 + 512
OWN = 4096
R = HALO + OWN
TILES = [(512 * i, 512) for i in range(13)]
OWN_T0 = 5
FPARTS = [(0, 8), (8, 7), (15, 7)]
EPS = 1e-5
NEG = -30000.0
TWO_PI = 6.283185307179586
PI = 3.141592653589793


class Grp:
    __slots__ = ("chan", "count")

    def __init__(self, chan):
        self.chan = chan
        self.count = 0


class Op:
    __slots__ = ("eng", "fn", "deps", "sig", "ticket", "grp", "isdma")

    def __init__(self, eng, fn):
        self.eng = eng
        self.fn = fn
        self.deps = []
        self.sig = False
        self.ticket = 0
        self.grp = None
        self.isdma = False


class Prog:
    ENGS = ("pe", "act", "dve", "pool", "sp")

    def __init__(self, nc):
        self.nc = nc
        self.eng_ops = {e: [] for e in self.ENGS}
        self.last_writer = {}
        self.readers = {}
        self.chan_cnt = {}
        self.last_dma = {}
        self.nbar = 0

    def _deps(self, o, reads, writes):
        deps = {}
        lw = self.last_writer
        rd = self.readers
        for b in reads:
            w = lw.get(b)
            if w is not None:
                deps[id(w)] = w
        for b in writes:
            w = lw.get(b)
            if w is not None:
                deps[id(w)] = w
            r = rd.get(b)
            if r:
                for x in r.values():
                    deps[id(x)] = x
        deps.pop(id(o), None)
        for b in writes:
            lw[b] = o
            rd[b] = {}
        key = ("dma", id(o)) if o.isdma else o.eng
        for b in reads:
            r = rd.get(b)
            if r is None:
                r = rd[b] = {}
            r[key] = o
        for d in deps.values():
            if d.eng == "pe" and o.eng == "pe" and not d.isdma and not o.isdma:
                continue
            d.sig = True
            o.deps.append(d)

    def op(self, eng, fn, reads=(), writes=()):
        o = Op(eng, fn)
        self._deps(o, reads, writes)
        self.eng_ops[eng].append(o)
        return o

    def dma(self, eng, fn, reads=(), writes=(), chan=None):
        o = Op(eng, fn)
        o.isdma = True
        grp = Grp(chan)
        o.grp = grp
        self._deps(o, reads, writes)
        c = self.chan_cnt.get(chan, 0) + 1
        self.chan_cnt[chan] = c
        grp.count = c
        self.last_dma[chan] = o
        self.eng_ops[eng].append(o)
        return o

    def barrier(self):
        self.nbar += 1
        firsts = []
        for e in self.ENGS:
            o = Op(e, lambda h: h.drain())
            o.sig = True
            self.eng_ops[e].append(o)
            firsts.append(o)
        dmas = list(self.last_dma.values())
        for e in self.ENGS:
            o = Op(e, lambda h: h.nop())
            for f in firsts:
                if f.eng != e:
                    o.deps.append(f)
            o.deps.extend(dmas)
            self.eng_ops[e].append(o)

    def emit(self):
        nc = self.nc
        sems = {e: nc.alloc_semaphore("s_" + e) for e in self.ENGS}
        csems = {c: nc.alloc_semaphore("c_" + str(c)) for c in self.chan_cnt}
        for e in self.ENGS:
            t = 0
            for o in self.eng_ops[e]:
                if o.isdma:
                    continue
                if o.sig:
                    t += 1
                    o.ticket = t

        def run(e, h):
            seen = {}
            for o in self.eng_ops[e]:
                for d in o.deps:
                    if d.isdma:
                        key = ("c", d.grp.chan)
                        val = d.grp.count * 16
                        s = csems[d.grp.chan]
                    else:
                        key = d.eng
                        val = d.ticket
                        s = sems[d.eng]
                    if seen.get(key, 0) < val:
                        h.wait_ge(s, val)
                        seen[key] = val
                ins = o.fn(h)
                if o.isdma:
                    ins.then_inc(csems[o.grp.chan], 16)
                elif o.sig:
                    ins.then_inc(sems[e], 1)

        with nc.Block() as block:
            @block.tensor
            def _(h):
                run("pe", h)

            @block.scalar
            def _(h):
                run("act", h)

            @block.vector
            def _(h):
                run("dve", h)

            @block.gpsimd
            def _(h):
                run("pool", h)

            @block.sync
            def _(h):
                run("sp", h)


class K:
    pass


def build(stop_after=None, dbg=False):
    nc = bass.Bass("TRN2", target_bir_lowering=False)
    P = Prog(nc)
    k = K()

    def din(name, shape, dt=F32):
        return nc.dram_tensor(name, list(shape), dt, kind="ExternalInput").ap()

    def dscr(name, shape, dt=F32):
        return nc.dram_tensor(name, list(shape), dt, kind=("ExternalOutput" if dbg else "Internal")).ap()

    xin = din("xin", [128, 8, R])
    cvec = din("cvec", [128, 8])
    posd = din("pos", [1, R], I32)
    flagd = din("flag", [128, 1])
    maskFd = din("maskF", [128, 256])
    maskNd = din("maskN", [128, 256])
    seld = din("sel", [128, 128])
    identd = din("ident", [128, 128])
    ropecd = din("ropec", [128, 2])
    adaw = din("adaw", [4, 1024, 9216])
    adab = din("adab", [128, 4, 72])
    kvadaw = din("kvadaw", [1024, 2048])
    kvadab = din("kvadab", [128, 16])
    normg = din("normg", [128, 96])
    kvnormg = din("kvnormg", [128, 8])
    fnormg = din("fnormg", [128, 8])
    f_win = [din("ffn1_w_in", [4, 1024, 5632]), din("ffn2_w_in", [4, 1024, 5632])]
    f_wout = [din("ffn1_w_out", [4, 2816, 1024]), din("ffn2_w_out", [4, 2816, 1024])]
    cw_in = din("conv_w_in", [2, 1024, 3072])
    cw_out = din("conv_w_out", [2, 1024, 1024])
    convwd = din("convw", [128, 48])
    wkv = din("w_kv", [1024, 3072])
    wksw = din("wk_sw", [1024, 1536])
    wq = din("attn_w_q", [2, 1024, 1536])
    wqsw = din("wq_sw", [2, 1024, 1536])
    wo = din("attn_w_o", [2, 512, 1024])
    outd = nc.dram_tensor("out", [128, 8, OWN], F32, kind="ExternalOutput").ap()

    X = dscr("Xs", [128, 8, R])
    H = dscr("Hs", [128, 8, R], BF16)
    KT = dscr("KTs", [128, 12, R], BF16)
    Vd = dscr("Vs", [12, R, 128], BF16)
    QT = dscr("QTs", [128, 12, OWN], BF16)
    MIX = dscr("MIXs", [128, 4, OWN], BF16)

    WB = 24576
    wb = [nc.alloc_sbuf_tensor("s_wb%d" % i, [128, WB], BF16) for i in range(2)]
    wsm = nc.alloc_sbuf_tensor("s_wsm", [128, 8192], BF16)
    wstg = nc.alloc_sbuf_tensor("s_wstg", [128, 2, 1024], F32)
    xs = nc.alloc_sbuf_tensor("s_xs", [128, 2, 8, 512], F32)
    ones = nc.alloc_sbuf_tensor("s_ones", [128, 128], F32)
    cond = nc.alloc_sbuf_tensor("s_cond", [128, 8], F32)
    mods = nc.alloc_sbuf_tensor("s_mods", [128, 4, 72], F32)
    kvm = nc.alloc_sbuf_tensor("s_kvm", [128, 16], F32)
    ng = nc.alloc_sbuf_tensor("s_ng", [128, 96], F32)
    kvng = nc.alloc_sbuf_tensor("s_kvng", [128, 8], F32)
    fng = nc.alloc_sbuf_tensor("s_fng", [128, 8], F32)
    gsb = nc.alloc_sbuf_tensor("s_gsb", [128, 14, 8], F32)
    gate = nc.alloc_sbuf_tensor("s_gate", [128, 12, 8], F32)
    cwt = nc.alloc_sbuf_tensor("s_cwt", [128, 48], F32)
    flag = nc.alloc_sbuf_tensor("s_flag", [128, 1], F32)
    ropec = nc.alloc_sbuf_tensor("s_ropec", [128, 2], F32)
    tmpb = nc.alloc_sbuf_tensor("s_tmpb", [128, 96], F32)
    epsb = nc.alloc_sbuf_tensor("s_epsb", [128, 1], F32)
    hs = nc.alloc_sbuf_tensor("s_hs", [128, 8, 512], BF16)
    gt = nc.alloc_sbuf_tensor("s_gt", [128, 8, 512], BF16)
    tm = nc.alloc_sbuf_tensor("s_tm", [128, 4, 512], F32)
    rstd = nc.alloc_sbuf_tensor("s_rstd", [128, 512], F32)
    vt = nc.alloc_sbuf_tensor("s_vt", [128, 2, 516], F32)
    vh = nc.alloc_sbuf_tensor("s_vh", [128, 8, 2], F32)
    ropeC = nc.alloc_sbuf_tensor("s_ropeC", [128, 512], F32)
    ropeS = nc.alloc_sbuf_tensor("s_ropeS", [128, 512], F32)
    posi = nc.alloc_sbuf_tensor("s_posi", [128, 512], I32)
    fout = nc.alloc_sbuf_tensor("s_fout", [128, 8, 512], F32)
    ktile = fout[:].bitcast(BF16)[:, :, :].rearrange("p a (b t) -> p (a b) t", t=512)[:, 0:12, :]
    ps = [nc.alloc_psum_tensor("p_ps%d" % i, [128, 512], F32) for i in range(7)]
    psb = nc.alloc_psum_tensor("p_psb", [128, 1024], BF16)
    print("sbuf remaining after alloc", nc.sbuf_bytes_remaining)

    state = {"piece_i": 0, "pending": [], "npass": 0}

    def load_piece(src, dst, dstname):
        i = state["piece_i"]
        state["piece_i"] = i + 1
        s = i % 2
        n = src.shape[-1]
        P.dma("act", lambda h: h.dma_start(out=wstg[:, s, 0:n], in_=src), writes=["wstg%d" % s], chan="wstg%d" % s)
        P.op("pool", lambda h: h.tensor_copy(dst, wstg[:, s, 0:n]), reads=["wstg%d" % s], writes=[dstname])

    def pump(npieces):
        for _ in range(npieces):
            if not state["pending"]:
                return
            load_piece(*state["pending"].pop(0))

    def rows(w2d, kk):
        return w2d[kk * 128:(kk + 1) * 128, :]

    def queue_weights(specs):
        for (blk, c0, ncol, dst, name) in specs:
            o = 0
            while o < ncol:
                n = min(1024, ncol - o)
                state["pending"].append((blk[:, c0 + o:c0 + o + n], dst[:, o:o + n], name))
                o += n

    def ld(dst, src, name):
        P.dma("sp", lambda h: h.dma_start(out=dst, in_=src), writes=[name], chan="c_" + name)

    ld(cond[:], cvec, "cond")
    ld(ng[:], normg, "ng")
    ld(kvng[:], kvnormg, "kvng")
    ld(fng[:], fnormg, "fng")
    ld(cwt[:], convwd, "cwt")
    ld(flag[:], flagd, "flag")
    ld(ropec[:], ropecd, "ropec")
    ld(mods[:], adab, "mods")
    ld(kvm[:], kvadab, "kvm")
    P.op("dve", lambda h: h.memset(ones[:], 1.0), writes=["ones"])
    P.op("dve", lambda h: h.memset(epsb[:], EPS), writes=["epsb"])
    P.op("act", lambda h: h.activation(out=cond[:], in_=cond[:], func=AF.Silu), reads=["cond"], writes=["cond"])
    wbf = [w[:].bitcast(F32) for w in wb]
    condrep = fout[:, 0:2, :].rearrange("p a t -> p (a t)")
    identf0 = vt[:, 0, 0:128]
    P.dma("sp", lambda h: h.dma_start(out=identf0, in_=identd), writes=["vt0"], chan="c_identf")
    for kk in range(8):
        P.op("dve", lambda h, kk=kk: h.tensor_scalar(out=condrep[:, kk * 128:(kk + 1) * 128], in0=ones[:], scalar1=cond[:, kk:kk + 1],
                                                     scalar2=None, op0=ALU.mult), reads=["ones", "cond"], writes=["condrep"])
    nblk = 0
    for l in range(5):
        ncol = 9216 if l < 4 else 2048
        src_all = (adaw[l] if l < 4 else kvadaw).rearrange("(k p) n -> p k n", p=128)
        for cb in range(ncol // 512):
            s_ = nblk % 2
            pst = ps[nblk % 2]
            pstn = "ps%d" % (nblk % 2)
            nblk += 1
            stg = wbf[s_][:, 0:4096].rearrange("p (k n) -> p k n", k=8)
            P.dma("sp", lambda h, stg=stg, src_all=src_all, cb=cb: h.dma_start(out=stg, in_=src_all[:, :, cb * 512:(cb + 1) * 512]),
                  writes=["wb%d" % s_], chan="wbl%d" % s_)
            for kk in range(8):
                P.op("pe", lambda h, stg=stg, kk=kk, pst=pst: h.matmul(pst[:, :], condrep[:, kk * 128:(kk + 1) * 128], stg[:, kk, :],
                                                                      start=(kk == 0), stop=(kk == 7)),
                     reads=["wb%d" % s_, "condrep"], writes=[pstn])
            for jj in range(4):
                j = cb * 4 + jj
                dstc = (mods[:, l, j:j + 1] if l < 4 else kvm[:, j:j + 1])
                dname = "mods" if l < 4 else "kvm"
                tcol = tmpb[:, (j % 8):(j % 8) + 1]
                P.op("dve", lambda h, pst=pst, jj=jj: h.tensor_tensor(out=tm[:, 0, 0:128], in0=pst[:, jj * 128:(jj + 1) * 128], in1=identf0,
                                                                      op=ALU.mult), reads=[pstn, "vt0"], writes=["tm0"])
                P.op("dve", lambda h, tcol=tcol: h.tensor_reduce(out=tcol, in_=tm[:, 0, 0:128], axis=AX.X, op=ALU.add),
                     reads=["tm0"], writes=["tmpb"])
                P.op("dve", lambda h, dstc=dstc, tcol=tcol: h.tensor_tensor(out=dstc, in0=dstc, in1=tcol, op=ALU.add),
                     reads=["tmpb", dname], writes=[dname])
    for l in range(4):
        for i in range(3):
            sc = mods[:, l, (3 * i + 1) * 8:(3 * i + 2) * 8]
            gg = mods[:, l, (3 * i + 2) * 8:(3 * i + 3) * 8]
            idx = l * 3 + i
            P.op("dve", lambda h, sc=sc, idx=idx: h.scalar_tensor_tensor(
                out=gsb[:, idx, :], in0=sc, scalar=1.0, in1=ng[:, idx * 8:(idx + 1) * 8], op0=ALU.add, op1=ALU.mult),
                reads=["mods", "ng"], writes=["gsb"])
            mul = 1.0 if i == 1 else 0.5
            P.op("dve", lambda h, gg=gg, idx=idx, mul=mul: h.tensor_scalar(
                out=gate[:, idx, :], in0=gg, scalar1=1.0, scalar2=mul, op0=ALU.add, op1=ALU.mult),
                reads=["mods"], writes=["gate"])
    P.op("dve", lambda h: h.scalar_tensor_tensor(out=gsb[:, 12, :], in0=kvm[:, 8:16], scalar=1.0, in1=kvng[:],
                                                 op0=ALU.add, op1=ALU.mult), reads=["kvm", "kvng"], writes=["gsb"])
    P.op("dve", lambda h: h.tensor_copy(gsb[:, 13, :], fng[:]), reads=["fng"], writes=["gsb"])
    P.barrier()

    def load_x(ti, slot, src=None):
        u0, n = TILES[ti]
        s_ap = (X if src is None else src)[:, :, u0:u0 + n]
        P.dma("sp", lambda h: h.dma_start(out=xs[:, slot, :, 0:n], in_=s_ap), reads=["X%d" % ti],
              writes=["xs%d" % slot], chan="xs%d" % slot)

    def store_x(ti, slot):
        u0, n = TILES[ti]
        P.dma("sp", lambda h: h.dma_start(out=X[:, :, u0:u0 + n], in_=xs[:, slot, :, 0:n]), reads=["xs%d" % slot],
              writes=["X%d" % ti], chan="xst%d" % slot)

    def norm_tile(slot, n, gidx, shift, out_fn):
        xn = "xs%d" % slot
        for c in range(8):
            t = c % 2
            P.op("act", lambda h, c=c, t=t: h.activation(out=tm[:, t, 0:n], in_=xs[:, slot, c, 0:n], func=AF.Square),
                 reads=[xn], writes=["tm%d" % t])
            P.op("pe", lambda h, c=c, t=t: h.matmul(ps[6][:, 0:n], ones[:], tm[:, t, 0:n], start=(c == 0), stop=(c == 7)),
                 reads=["ones", "tm%d" % t], writes=["ps6"])
        P.op("act", lambda h: h.activation(out=rstd[:, 0:n], in_=ps[6][:, 0:n], func=AF.Sqrt, bias=epsb[:, 0:1], scale=1.0 / D),
             reads=["ps6", "epsb"], writes=["rstd"])
        P.op("dve", lambda h: h.reciprocal(rstd[:, 0:n], rstd[:, 0:n]), reads=["rstd"], writes=["rstd"])
        for c in range(8):
            t = 2 + c % 2
            P.op("dve", lambda h, c=c, t=t: h.tensor_tensor(out=tm[:, t, 0:n], in0=xs[:, slot, c, 0:n], in1=rstd[:, 0:n],
                                                            op=ALU.mult), reads=[xn, "rstd"], writes=["tm%d" % t])
            dst, dname = out_fn(c)
            if shift is not None:
                P.op("act", lambda h, c=c, t=t, dst=dst: h.activation(
                    out=dst, in_=tm[:, t, 0:n], func=AF.Identity, bias=shift[:, c:c + 1], scale=gsb[:, gidx, c:c + 1]),
                    reads=["tm%d" % t, "gsb", "mods", "kvm"], writes=[dname])
            else:
                P.op("act", lambda h, c=c, t=t, dst=dst: h.activation(
                    out=dst, in_=tm[:, t, 0:n], func=AF.Identity, scale=gsb[:, gidx, c:c + 1]),
                    reads=["tm%d" % t, "gsb"], writes=[dname])

    def hs_out(n):
        return lambda c: (hs[:, c, 0:n], "hs")

    def store_h(ti, n):
        u0 = TILES[ti][0]
        P.dma("sp", lambda h: h.dma_start(out=H[:, :, u0:u0 + n], in_=hs[:, :, 0:n]), reads=["hs"], writes=["H%d" % ti],
              chan="hst")

    def load_h(ti, n):
        u0 = TILES[ti][0]
        P.dma("sp", lambda h: h.dma_start(out=hs[:, :, 0:n], in_=H[:, :, u0:u0 + n]), reads=["H%d" % ti], writes=["hs"],
              chan="hld")

    def proj_out(wout_v, nk, src_fn, slot, n, gidx):
        for m in range(8):
            pb = ps[4 + m % 2]
            pn = "ps%d" % (4 + m % 2)
            for kk in range(nk):
                sap, sname = src_fn(kk)
                P.op("pe", lambda h, m=m, kk=kk, pb=pb, sap=sap: h.matmul(
                    pb[:, 0:n], wout_v[:, kk, m * 128:(m + 1) * 128], sap, start=(kk == 0), stop=(kk == nk - 1)),
                    reads=[state["wname_out"], sname], writes=[pn])
            P.op("dve", lambda h, m=m, pb=pb: h.scalar_tensor_tensor(
                out=xs[:, slot, m, 0:n], in0=pb[:, 0:n], scalar=gate[:, gidx, m:m + 1], in1=xs[:, slot, m, 0:n],
                op0=ALU.mult, op1=ALU.add), reads=[pn, "gate", "xs%d" % slot], writes=["xs%d" % slot])

    def begin_pass(tiles, first_src=None):
        state["npass"] += 1
        state["tiles"] = tiles
        state["first_src"] = first_src

    def ffn_weights(l, f, j, buf):
        c0, ncj = FPARTS[j]
        specs = []
        win = f_win[f][l]
        wout = f_wout[f][l]
        w_in_v = wb[buf][:, 0:8 * 2 * ncj * 128].rearrange("p (k n) -> p k n", k=8)
        w_out_v = wb[buf][:, 16384:16384 + ncj * 1024].rearrange("p (k n) -> p k n", k=ncj)
        for kk in range(8):
            blk = rows(win, kk)
            specs.append((blk, c0 * 128, ncj * 128, w_in_v[:, kk, 0:ncj * 128], "wb%d" % buf))
            specs.append((blk, DFF + c0 * 128, ncj * 128, w_in_v[:, kk, ncj * 128:2 * ncj * 128], "wb%d" % buf))
        for kk in range(ncj):
            specs.append((rows(wout, c0 + kk), 0, 1024, w_out_v[:, kk, :], "wb%d" % buf))
        return specs, (w_in_v, w_out_v, ncj)

    def ffn_pass(l, f, j, buf, wv, tiles, src0=None):
        w_in_v, w_out_v, ncj = wv
        gidx = l * 3 + (0 if f == 0 else 2)
        shift = mods[:, l, (0 if f == 0 else 6) * 8:(0 if f == 0 else 6) * 8 + 8]
        wname = "wb%d" % buf
        state["wname_out"] = wname
        npump = (len(state["pending"]) + len(tiles) - 1) // max(1, len(tiles)) + 1
        load_x(tiles[0], 0, src0 if j == 0 else None)

        def body(ii, ti, n, slot):
            if ii + 1 < len(tiles):
                load_x(tiles[ii + 1], 1 - slot, src0 if j == 0 else None)
            pump(npump)
            if j == 0:
                norm_tile(slot, n, gidx, shift, hs_out(n))
                store_h(ti, n)
            else:
                load_h(ti, n)
            for cc in range(ncj):
                pa, pbb = ps[(cc % 2) * 2], ps[(cc % 2) * 2 + 1]
                na, nb_ = "ps%d" % ((cc % 2) * 2), "ps%d" % ((cc % 2) * 2 + 1)
                for (pp, pn, off) in ((pa, na, 0), (pbb, nb_, ncj * 128)):
                    for kk in range(8):
                        P.op("pe", lambda h, pp=pp, kk=kk, cc=cc, off=off: h.matmul(
                            pp[:, 0:n], w_in_v[:, kk, off + cc * 128:off + (cc + 1) * 128], hs[:, kk, 0:n],
                            start=(kk == 0), stop=(kk == 7)), reads=[wname, "hs"], writes=[pn])
                t = cc % 2
                P.op("act", lambda h, pa=pa, t=t: h.activation(out=tm[:, t, 0:n], in_=pa[:, 0:n], func=AF.Silu),
                     reads=[na], writes=["tm%d" % t])
                P.op("dve", lambda h, pbb=pbb, t=t, cc=cc: h.tensor_tensor(out=gt[:, cc, 0:n], in0=pbb[:, 0:n],
                                                                           in1=tm[:, t, 0:n], op=ALU.mult),
                     reads=[nb_, "tm%d" % t], writes=["gt%d" % cc])
            proj_out(w_out_v, ncj, lambda kk: (gt[:, kk, 0:n], "gt%d" % kk), slot, n, gidx)
            store_x(ti, slot)

        for ii, ti in enumerate(tiles):
            body(ii, ti, TILES[ti][1], ii % 2)
        pump(10 ** 6)
        P.barrier()

    def conv_weights(l, buf):
        w_in_v = wb[buf][:, 0:8 * 3072].rearrange("p (k n) -> p k n", k=8)
        w_out_v = wsm[:, 0:8192].rearrange("p (k n) -> p k n", k=8)
        specs = []
        for kk in range(8):
            specs.append((rows(cw_in[l], kk), 0, 3072, w_in_v[:, kk, :], "wb%d" % buf))
        for kk in range(8):
            specs.append((rows(cw_out[l], kk), 0, 1024, w_out_v[:, kk, :], "wsm"))
        return specs, (w_in_v, w_out_v)

    def conv_pass(l, buf, wv, tiles):
        w_in_v, w_out_v = wv
        gidx = l * 3 + 1
        shift = mods[:, l, 24:32]
        wname = "wb%d" % buf
        state["wname_out"] = "wsm"
        npump = (len(state["pending"]) + len(tiles) - 1) // max(1, len(tiles)) + 1
        P.op("dve", lambda h: h.memset(vh[:], 0.0), writes=["vh"])
        load_x(tiles[0], 0)

        def body(ii, ti, n, slot):
            if ii + 1 < len(tiles):
                load_x(tiles[ii + 1], 1 - slot)
            pump(npump)
            if ti == OWN_T0:
                P.op("dve", lambda h: h.tensor_scalar(out=vh[:], in0=vh[:], scalar1=flag[:, 0:1], scalar2=None, op0=ALU.mult),
                     reads=["vh", "flag"], writes=["vh"])
            norm_tile(slot, n, gidx, shift, hs_out(n))
            for c in range(8):
                s3 = (c % 2) * 3
                pB, pC, pU = ps[s3 % 6], ps[(s3 + 1) % 6], ps[(s3 + 2) % 6]
                nB, nC, nU = "ps%d" % (s3 % 6), "ps%d" % ((s3 + 1) % 6), "ps%d" % ((s3 + 2) % 6)
                for (pp, pn, off) in ((pB, nB, 0), (pC, nC, 1024), (pU, nU, 2048)):
                    for kk in range(8):
                        P.op("pe", lambda h, pp=pp, kk=kk, c=c, off=off: h.matmul(
                            pp[:, 0:n], w_in_v[:, kk, off + c * 128:off + (c + 1) * 128], hs[:, kk, 0:n],
                            start=(kk == 0), stop=(kk == 7)), reads=[wname, "hs"], writes=[pn])
                t = c % 2
                vn = "vt%d" % t
                P.op("act", lambda h, pC=pC, t=t: h.activation(out=tm[:, t, 0:n], in_=pC[:, 0:n], func=AF.Identity),
                     reads=[nC], writes=["tm%d" % t])
                P.op("dve", lambda h, c=c, t=t: h.tensor_copy(vt[:, t, 0:2], vh[:, c, :]), reads=["vh"], writes=[vn])
                P.op("dve", lambda h, pU=pU, t=t: h.tensor_tensor(out=vt[:, t, 2:2 + n], in0=pU[:, 0:n], in1=tm[:, t, 0:n],
                                                                  op=ALU.mult), reads=[nU, "tm%d" % t], writes=[vn])
                P.op("dve", lambda h, c=c, t=t: h.tensor_copy(vh[:, c, :], vt[:, t, n:n + 2]), reads=[vn], writes=["vh"])
                w0 = cwt[:, l * 24 + 0 * 8 + c:l * 24 + 0 * 8 + c + 1]
                w1 = cwt[:, l * 24 + 1 * 8 + c:l * 24 + 1 * 8 + c + 1]
                w2 = cwt[:, l * 24 + 2 * 8 + c:l * 24 + 2 * 8 + c + 1]
                t2 = 2 + t
                P.op("dve", lambda h, t=t, t2=t2, w2=w2: h.tensor_scalar(out=tm[:, t2, 0:n], in0=vt[:, t, 2:2 + n], scalar1=w2,
                                                                         scalar2=None, op0=ALU.mult),
                     reads=[vn, "cwt"], writes=["tm%d" % t2])
                P.op("dve", lambda h, t=t, t2=t2, w1=w1: h.scalar_tensor_tensor(
                    out=tm[:, t2, 0:n], in0=vt[:, t, 1:1 + n], scalar=w1, in1=tm[:, t2, 0:n], op0=ALU.mult, op1=ALU.add),
                    reads=[vn, "cwt", "tm%d" % t2], writes=["tm%d" % t2])
                P.op("dve", lambda h, t=t, t2=t2, w0=w0: h.scalar_tensor_tensor(
                    out=tm[:, t2, 0:n], in0=vt[:, t, 0:n], scalar=w0, in1=tm[:, t2, 0:n], op0=ALU.mult, op1=ALU.add),
                    reads=[vn, "cwt", "tm%d" % t2], writes=["tm%d" % t2])
                P.op("dve", lambda h, pB=pB, t2=t2, c=c: h.tensor_tensor(out=gt[:, c, 0:n], in0=pB[:, 0:n], in1=tm[:, t2, 0:n],
                                                                         op=ALU.mult),
                     reads=[nB, "tm%d" % t2], writes=["gt%d" % c])
            proj_out(w_out_v, 8, lambda kk: (gt[:, kk, 0:n], "gt%d" % kk), slot, n, gidx)
            store_x(ti, slot)

        for ii, ti in enumerate(tiles):
            body(ii, ti, TILES[ti][1], ii % 2)
        pump(10 ** 6)
        P.barrier()

    def rope_tile(u0, n):
        P.dma("sp", lambda h: h.dma_start(out=posi[:, 0:n], in_=posd[0:1, u0:u0 + n].partition_broadcast(128)),
              writes=["posi"], chan="posi")
        P.op("dve", lambda h: h.tensor_copy(tm[:, 0, 0:n], posi[:, 0:n]), reads=["posi"], writes=["tm0"])
        P.op("dve", lambda h: h.tensor_scalar(out=tm[:, 0, 0:n], in0=tm[:, 0, 0:n], scalar1=ropec[:, 0:1], scalar2=1.0 / TWO_PI,
                                              op0=ALU.mult, op1=ALU.mult), reads=["tm0", "ropec"], writes=["tm0"])
        for dst, off in ((1, 0.0), (2, 0.25)):
            dn = "tm%d" % dst
            P.op("dve", lambda h, dst=dst, off=off: h.tensor_scalar(out=tm[:, dst, 0:n], in0=tm[:, 0, 0:n], scalar1=off, scalar2=None,
                                                                    op0=ALU.add), reads=["tm0"], writes=[dn])
            P.op("dve", lambda h, dst=dst: h.tensor_copy(posi[:, 0:n], tm[:, dst, 0:n]), reads=[dn], writes=["posi"])
            P.op("dve", lambda h: h.tensor_copy(tm[:, 3, 0:n], posi[:, 0:n]), reads=["posi"], writes=["tm3"])
            P.op("dve", lambda h, dst=dst: h.tensor_tensor(out=tm[:, dst, 0:n], in0=tm[:, dst, 0:n], in1=tm[:, 3, 0:n], op=ALU.subtract),
                 reads=[dn, "tm3"], writes=[dn])
            P.op("dve", lambda h, dst=dst: h.tensor_scalar(out=tm[:, 3, 0:n], in0=tm[:, dst, 0:n], scalar1=0.5, scalar2=None,
                                                           op0=ALU.is_gt), reads=[dn], writes=["tm3"])
            P.op("dve", lambda h, dst=dst: h.tensor_tensor(out=tm[:, dst, 0:n], in0=tm[:, dst, 0:n], in1=tm[:, 3, 0:n], op=ALU.subtract),
                 reads=[dn, "tm3"], writes=[dn])
        SC = TWO_PI * (1.0 - 1e-6)
        P.op("act", lambda h: h.activation(out=ropeS[:, 0:n], in_=tm[:, 1, 0:n], func=AF.Sin, scale=SC), reads=["tm1"], writes=["ropeS"])
        P.op("act", lambda h: h.activation(out=ropeC[:, 0:n], in_=tm[:, 2, 0:n], func=AF.Sin, scale=SC), reads=["tm2"], writes=["ropeC"])
        P.op("dve", lambda h: h.tensor_scalar(out=ropeS[:, 0:n], in0=ropeS[:, 0:n], scalar1=ropec[:, 1:2], scalar2=None,
                                              op0=ALU.mult), reads=["ropeS", "ropec"], writes=["ropeS"])

    def qk_weights(wmat, wsw, buf):
        w1 = wb[buf][:, 0:8 * 1536].rearrange("p (k n) -> p k n", k=8)
        w2 = wb[buf][:, 8 * 1536:16 * 1536].rearrange("p (k n) -> p k n", k=8)
        specs = []
        for kk in range(8):
            specs.append((rows(wmat, kk), 0, 1536, w1[:, kk, :], "wb%d" % buf))
            specs.append((rows(wsw, kk), 0, 1536, w2[:, kk, :], "wb%d" % buf))
        return specs, (w1, w2)

    def qk_pass(buf, wv, tiles, gidx, shift, dst, dst_off, dname, save_h):
        w1, w2 = wv
        wname = "wb%d" % buf
        npump = (len(state["pending"]) + len(tiles) - 1) // max(1, len(tiles)) + 1
        load_x(tiles[0], 0)

        def body(ii, ti, u0, n, slot):
            if ii + 1 < len(tiles):
                load_x(tiles[ii + 1], 1 - slot)
            pump(npump)
            rope_tile(u0, n)
            norm_tile(slot, n, gidx, shift, hs_out(n))
            if save_h:
                store_h(ti, n)
            for c in range(12):
                pa, pbb = ps[(c % 2) * 2], ps[(c % 2) * 2 + 1]
                na, nb_ = "ps%d" % ((c % 2) * 2), "ps%d" % ((c % 2) * 2 + 1)
                for (pp, pn, wv_) in ((pa, na, w1), (pbb, nb_, w2)):
                    for kk in range(8):
                        P.op("pe", lambda h, pp=pp, kk=kk, c=c, wv_=wv_: h.matmul(
                            pp[:, 0:n], wv_[:, kk, c * 128:(c + 1) * 128], hs[:, kk, 0:n],
                            start=(kk == 0), stop=(kk == 7)), reads=[wname, "hs"], writes=[pn])
                t = c % 2
                P.op("dve", lambda h, pa=pa, t=t: h.tensor_tensor(out=tm[:, t, 0:n], in0=pa[:, 0:n], in1=ropeC[:, 0:n], op=ALU.mult),
                     reads=[na, "ropeC"], writes=["tm%d" % t])
                P.op("dve", lambda h, pbb=pbb, t=t: h.tensor_tensor(out=tm[:, 2 + t, 0:n], in0=pbb[:, 0:n], in1=ropeS[:, 0:n],
                                                                    op=ALU.mult), reads=[nb_, "ropeS"], writes=["tm%d" % (2 + t)])
                P.op("dve", lambda h, t=t, c=c: h.tensor_tensor(out=ktile[:, c, 0:n], in0=tm[:, t, 0:n], in1=tm[:, 2 + t, 0:n],
                                                                op=ALU.add), reads=["tm%d" % t, "tm%d" % (2 + t)], writes=["ktile"])
            o0 = u0 - dst_off
            P.dma("sp", lambda h, o0=o0, n=n: h.dma_start(out=dst[:, :, o0:o0 + n], in_=ktile[:, :, 0:n]), reads=["ktile"],
                  writes=["%s%d" % (dname, ti)], chan="ktst")

        for ii, ti in enumerate(tiles):
            body(ii, ti, TILES[ti][0], TILES[ti][1], ii % 2)
        pump(10 ** 6)
        P.barrier()

    vtile = ktile
    vtv = ktile[:, 0:3, :]

    def v_weights(buf):
        w1 = wb[buf][:, 0:8 * 1536].rearrange("p (k n) -> p k n", k=8)
        specs = []
        for kk in range(8):
            specs.append((rows(wkv, kk), 1536, 1536, w1[:, kk, :], "wb%d" % buf))
        return specs, (w1,)

    def v_pass(buf, wv, tiles):
        (w1,) = wv
        wname = "wb%d" % buf
        npump = (len(state["pending"]) + len(tiles) - 1) // max(1, len(tiles)) + 1
        def body(ti, u0, n):
            pump(npump)
            load_h(ti, n)
            for s in range(n // 128):
                for cb in range(3):
                    pp = ps[cb % 2]
                    pn = "ps%d" % (cb % 2)
                    for kk in range(8):
                        P.op("pe", lambda h, pp=pp, kk=kk, s=s, cb=cb: h.matmul(
                            pp[:, :], hs[:, kk, s * 128:(s + 1) * 128], w1[:, kk, cb * 512:(cb + 1) * 512],
                            start=(kk == 0), stop=(kk == 7)), reads=[wname, "hs"], writes=[pn])
                    if cb % 2 == 0:
                        P.op("act", lambda h, pp=pp, cb=cb: h.activation(out=vtv[:, cb, :], in_=pp[:, :], func=AF.Identity),
                             reads=[pn], writes=["ktile"])
                    else:
                        P.op("dve", lambda h, pp=pp, cb=cb: h.tensor_copy(vtv[:, cb, :], pp[:, :]), reads=[pn], writes=["ktile"])
                t0 = u0 + s * 128
                P.dma("sp", lambda h, t0=t0: h.dma_start(
                    out=Vd[:, t0:t0 + 128, :].rearrange("c t f -> t c f"),
                    in_=vtv.rearrange("p a (b f) -> p (a b) f", f=128)), reads=["ktile"], writes=["V%d" % ti], chan="vst")

        for ti in tiles:
            body(ti, TILES[ti][0], TILES[ti][1])
        pump(10 ** 6)
        P.barrier()

    def att_pass(buf):
        KTw = wb[buf][:, 0:4096]
        Qg = wb[buf][:, 4096:6144]
        Vb = wb[buf][:, 6144:6144 + 32 * 128].rearrange("p (b f) -> p b f", f=128)
        Og = wb[buf][:, 10240:10240 + 3 * 2048].rearrange("p (g t) -> p g t", g=3)
        Mb0 = wb[buf][:, 16384:16384 + 4096].bitcast(F32)
        Db0 = wb[buf][:, 20480:20480 + 4096].bitcast(F32)
        ob = wb[1 - buf]

        def rowbuf(i):
            return ob[:, i * 4096:(i + 1) * 4096].bitcast(F32)
        MM = [Mb0, rowbuf(0), rowbuf(1)]
        DD = [Db0, rowbuf(2), rowbuf(3)]
        MxB = rowbuf(4)
        ZB = rowbuf(5)
        mixb = gt[:, 0:4, :].rearrange("p a t -> p (a t)")
        Sm = tm[:, 0:2, 0:256]
        Pt = hs[:, 0:2, 0:256]
        PT = hs[:, 2:4, 0:256]
        ST = tmpb
        maskN = tm[:, 2, 0:256]
        maskF = tm[:, 3, 0:256]
        sel = rstd[:, 0:128]
        ident = hs[:, 4, 0:128]
        identf = vt[:, 0, 0:128]
        rowt = ropeC
        rowt2 = ropeS
        wn = "wb%d" % buf
        P.dma("sp", lambda h: h.dma_start(out=maskN, in_=maskNd), writes=["tm2"], chan="mk")
        P.dma("sp", lambda h: h.dma_start(out=maskF, in_=maskFd), writes=["tm3"], chan="mk2")
        P.dma("sp", lambda h: h.dma_start(out=sel, in_=seld), writes=["rstd"], chan="mk3")
        P.dma("sp", lambda h: h.dma_start(out=identf, in_=identd), writes=["vt0"], chan="mk4")
        P.op("dve", lambda h: h.tensor_copy(ident, identf), reads=["vt0"], writes=["ident"])
        ps6b = ps[6][:].bitcast(BF16)

        def att_stages(g, d, s, r, span, first, par):
            mk, mkn = (maskF, "tm3") if first else (maskN, "tm2")
            q0 = s * span + r
            qsl = slice(q0, q0 + 127 * d + 1, d)
            ksl = slice(q0, q0 + 255 * d + 1, d)
            bp, bc = s * d + r, (s + 1) * d + r
            fs = slice(par * 256, par * 256 + 256)
            cb = par * 8
            sfx = "_%d" % par

            def stA():
                for hh in range(2):
                    hp = slice(hh * 64, hh * 64 + 64)
                    bi = hh if par == 0 else 4 + hh
                    P.op("pe", lambda h, hp=hp, bi=bi: h.matmul(ps[bi][:, 0:256], Qg[hp, qsl], KTw[hp, ksl], start=True, stop=True),
                         reads=["Qg", "KTw"], writes=["ps%d" % bi])

            def stB():
                for hh in range(2):
                    bi = hh if par == 0 else 4 + hh
                    smn = "Sm%d%s" % (hh, sfx)
                    P.op("dve", lambda h, hh=hh, bi=bi: h.tensor_tensor(out=tm[:, hh, fs], in0=ps[bi][:, 0:256], in1=mk, op=ALU.add),
                         reads=["ps%d" % bi, mkn], writes=[smn])
                    P.op("dve", lambda h, hh=hh: h.tensor_reduce(out=ST[:, cb + hh:cb + hh + 1], in_=tm[:, hh, fs], axis=AX.X, op=ALU.max),
                         reads=[smn], writes=["STm%d%s" % (hh, sfx)])
                    P.op("dve", lambda h, hh=hh: h.tensor_scalar(out=ST[:, cb + 4 + hh:cb + 5 + hh], in0=ST[:, cb + hh:cb + hh + 1],
                                                                 scalar1=-0.125, scalar2=None, op0=ALU.mult),
                         reads=["STm%d%s" % (hh, sfx)], writes=["STn%d%s" % (hh, sfx)])
                    P.op("act", lambda h, hh=hh: h.activation(out=hs[:, hh, fs], in_=tm[:, hh, fs], func=AF.Exp,
                                                              bias=ST[:, cb + 4 + hh:cb + 5 + hh], scale=0.125,
                                                              accum_out=ST[:, cb + 2 + hh:cb + 3 + hh]),
                         reads=[smn, "STn%d%s" % (hh, sfx)], writes=["Pt%d%s" % (hh, sfx), "STd%d%s" % (hh, sfx)])

            def stC():
                for hh in range(2):
                    pT = (psb if hh == 0 else ps6b)[:, 0:256]
                    pTn = "psb" if hh == 0 else "ps6"
                    for kc in range(2):
                        P.op("pe", lambda h, hh=hh, kc=kc, pT=pT: h.transpose(
                            pT[:, kc * 128:(kc + 1) * 128], hs[:, hh, par * 256 + kc * 128:par * 256 + (kc + 1) * 128], ident),
                            reads=["Pt%d%s" % (hh, sfx), "ident"], writes=[pTn])
                    P.op("act", lambda h, hh=hh, pT=pT: h.activation(out=hs[:, 2 + hh, fs], in_=pT, func=AF.Identity),
                         reads=[pTn], writes=["PT%d%s" % (hh, sfx)])

            def stD():
                for hh in range(2):
                    for kc, bb in ((0, bp), (1, bc)):
                        P.op("pe", lambda h, hh=hh, kc=kc, bb=bb: h.matmul(
                            ps[2][hh * 64:hh * 64 + 64, 0:128], Vb[:, bb, hh * 64:hh * 64 + 64],
                            hs[:, 2 + hh, par * 256 + kc * 128:par * 256 + (kc + 1) * 128],
                            start=(kc == 0), stop=(kc == 1)), reads=["Vb", "PT%d%s" % (hh, sfx)], writes=["ps2_%d" % hh])

            def stE():
                P.op("dve", lambda h: h.tensor_copy(Og[:, g, qsl], ps[2][:, 0:128]), reads=["ps2_0", "ps2_1"], writes=["Og"])
                P.op("pe", lambda h: h.transpose(ps[3][0:2, 0:128], ST[:, cb:cb + 2], identf),
                     reads=["STm0" + sfx, "STm1" + sfx, "vt0"], writes=["ps3"])
                P.op("pe", lambda h: h.transpose(ps[3][0:2, 128:256], ST[:, cb + 2:cb + 4], identf),
                     reads=["STd0" + sfx, "STd1" + sfx, "vt0"], writes=["ps3"])
                P.op("act", lambda h: h.activation(out=MM[g][0:2, qsl], in_=ps[3][0:2, 0:128], func=AF.Identity),
                     reads=["ps3"], writes=["Mb"])
                P.op("act", lambda h: h.activation(out=DD[g][0:2, qsl], in_=ps[3][0:2, 128:256], func=AF.Identity),
                     reads=["ps3"], writes=["Db"])

            return stA, stB, stC, stD, stE

        def att_group(g, d, span, nspan, first_ss):
            blocks = [(s, r) for s in range(nspan) for r in range(d)]
            st = [att_stages(g, d, s, r, span, (first_ss and s == 0), i % 2) for i, (s, r) in enumerate(blocks)]
            nbk = len(st)
            for i in range(nbk + 1):
                if i < nbk:
                    st[i][0]()
                if i >= 1:
                    st[i - 1][2]()
                if i < nbk:
                    st[i][1]()
                if i >= 1:
                    st[i - 1][3]()
                    st[i - 1][4]()

        cnt = 0
        for ss in range(2):
            o0 = ss * 2048
            for c in range(4):
                for g, d in ((0, 1), (1, 4), (2, 16)):
                    span = 128 * d
                    nspan = 2048 // span
                    ws = HALO + o0 - span
                    wl = span + 2048
                    ch = 4 * g + c
                    ktiles = sorted(set(u // 512 for u in (ws, ws + wl - 1)))
                    ktl = list(range(ktiles[0], ktiles[-1] + 1))
                    P.dma("sp", lambda h, ch=ch, ws=ws, wl=wl: h.dma_start(out=KTw[:, 0:wl], in_=KT[:, ch, ws:ws + wl]),
                          reads=["K%d" % t for t in ktl], writes=["KTw"], chan="ktw")
                    P.dma("sp", lambda h, ch=ch, o0=o0: h.dma_start(out=Qg[:, :], in_=QT[:, ch, o0:o0 + 2048]),
                          reads=["Q%d" % t for t in range(OWN_T0, 13)], writes=["Qg"], chan="qg")
                    nblk = (nspan + 1) * d
                    for sp_ in range(nspan + 1):
                        for rh in range(0, d, 8):
                            rn = min(8, d - rh)
                            src = Vd[ch, ws + sp_ * span:ws + (sp_ + 1) * span, :].rearrange("(k r) f -> k r f", r=d)[:, rh:rh + rn, :]
                            b0 = sp_ * d + rh
                            P.dma("sp", lambda h, src=src, b0=b0, rn=rn: h.dma_start(out=Vb[:, b0:b0 + rn, :], in_=src),
                                  reads=["V%d" % t for t in ktl], writes=["Vb"], chan="vb%d" % (cnt % 4))
                            cnt += 1
                    att_group(g, d, span, nspan, ss == 0)
                M0, M1, M2 = MM[0][0:2, :], MM[1][0:2, :], MM[2][0:2, :]
                D0, D1, D2 = DD[0][0:2, :], DD[1][0:2, :], DD[2][0:2, :]
                Mx = MxB[0:2, :]
                Z = ZB[0:2, :]
                P.op("dve", lambda h: h.tensor_tensor(out=Mx, in0=M0, in1=M1, op=ALU.max), reads=["Mb"], writes=["Mx"])
                P.op("dve", lambda h: h.tensor_tensor(out=Mx, in0=Mx, in1=M2, op=ALU.max), reads=["Mb", "Mx"], writes=["Mx"])
                for gi, (Mg, Dg) in enumerate(((M0, D0), (M1, D1), (M2, D2))):
                    P.op("dve", lambda h, Mg=Mg: h.tensor_tensor(out=Mg, in0=Mg, in1=Mx, op=ALU.subtract), reads=["Mb", "Mx"], writes=["Mb"])
                    P.op("act", lambda h, Mg=Mg: h.activation(out=Mg, in_=Mg, func=AF.Exp, scale=0.125), reads=["Mb"], writes=["Mb"])
                    P.op("dve", lambda h, Mg=Mg, Dg=Dg: h.tensor_tensor(out=Dg, in0=Dg, in1=Mg, op=ALU.mult), reads=["Mb", "Db"], writes=["Db"])
                P.op("dve", lambda h: h.tensor_tensor(out=Z, in0=D0, in1=D1, op=ALU.add), reads=["Db"], writes=["Z"])
                P.op("dve", lambda h: h.tensor_tensor(out=Z, in0=Z, in1=D2, op=ALU.add), reads=["Db", "Z"], writes=["Z"])
                P.op("dve", lambda h: h.reciprocal(Z, Z), reads=["Z"], writes=["Z"])
                for Mg in (M0, M1, M2):
                    P.op("dve", lambda h, Mg=Mg: h.tensor_tensor(out=Mg, in0=Mg, in1=Z, op=ALU.mult), reads=["Mb", "Z"], writes=["Mb"])
                for tt in range(4):
                    tsl = slice(tt * 512, (tt + 1) * 512)
                    for gi in range(3):
                        pF = ps[4 + gi % 2]
                        pFn = "ps%d" % (4 + gi % 2)
                        P.op("pe", lambda h, gi=gi, tsl=tsl, pF=pF: h.matmul(pF[:, :], sel[0:2, :], MM[gi][0:2, tsl],
                                                                             start=True, stop=True), reads=["Mb", "rstd"], writes=[pFn])
                        if gi == 0:
                            P.op("dve", lambda h, tsl=tsl, pF=pF: h.tensor_tensor(out=rowt[:, :], in0=pF[:, :], in1=Og[:, 0, tsl], op=ALU.mult),
                                 reads=[pFn, "Og"], writes=["rowt"])
                        else:
                            P.op("dve", lambda h, gi=gi, tsl=tsl, pF=pF: h.tensor_tensor(out=rowt2[:, :], in0=pF[:, :], in1=Og[:, gi, tsl],
                                                                                        op=ALU.mult), reads=[pFn, "Og"], writes=["rowt2"])
                            P.op("dve", lambda h: h.tensor_tensor(out=rowt[:, :], in0=rowt[:, :], in1=rowt2[:, :], op=ALU.add),
                                 reads=["rowt", "rowt2"], writes=["rowt"])
                    P.op("act", lambda h, tsl=tsl: h.activation(out=mixb[:, tsl], in_=rowt[:, :], func=AF.Identity), reads=["rowt"],
                         writes=["mixb"])
                P.dma("sp", lambda h, c=c, o0=o0: h.dma_start(out=MIX[:, c, o0:o0 + 2048], in_=mixb), reads=["mixb"],
                      writes=["MIX"], chan="mixst")
        P.barrier()

    def o_weights(j):
        w_out_v = wsm[:, 0:4096].rearrange("p (k n) -> p k n", k=4)
        specs = []
        for kk in range(4):
            specs.append((rows(wo[j], kk), 0, 1024, w_out_v[:, kk, :], "wsm"))
        return specs, (w_out_v,)

    def o_pass(l, wv, tiles):
        (w_out_v,) = wv
        gidx = l * 3 + 1
        state["wname_out"] = "wsm"
        npump = (len(state["pending"]) + len(tiles) - 1) // max(1, len(tiles)) + 1
        load_x(tiles[0], 0)

        def body(ii, ti, u0, n, slot):
            if ii + 1 < len(tiles):
                load_x(tiles[ii + 1], 1 - slot)
            pump(npump)
            o0 = u0 - HALO
            P.dma("sp", lambda h, o0=o0, n=n: h.dma_start(out=gt[:, 0:4, 0:n], in_=MIX[:, :, o0:o0 + n]), reads=["MIX"],
                  writes=["gt0", "gt1", "gt2", "gt3"], chan="mixld")
            proj_out(w_out_v, 4, lambda kk: (gt[:, kk, 0:n], "gt%d" % kk), slot, n, gidx)
            store_x(ti, slot)

        for ii, ti in enumerate(tiles):
            body(ii, ti, TILES[ti][0], TILES[ti][1], ii % 2)
        pump(10 ** 6)
        P.barrier()

    def fin_pass(tiles):
        load_x(tiles[0], 0)

        def body(ii, ti, u0, n, slot):
            if ii + 1 < len(tiles):
                load_x(tiles[ii + 1], 1 - slot)
            norm_tile(slot, n, 13, None, lambda c: (fout[:, c, 0:n], "fout"))
            o0 = u0 - HALO
            P.dma("sp", lambda h, o0=o0, n=n: h.dma_start(out=outd[:, :, o0:o0 + n], in_=fout[:, :, 0:n]), reads=["fout"],
                  writes=["OUT%d" % ti], chan="outst")

        for ii, ti in enumerate(tiles):
            body(ii, ti, TILES[ti][0], TILES[ti][1], ii % 2)
        P.barrier()

    print("sbuf remaining final", nc.sbuf_bytes_remaining)

    ALLT = list(range(13))
    KVT = list(range(1, 13))
    OWNT = list(range(OWN_T0, 13))
    sched = []
    for l in range(2):
        for f in range(2):
            if f == 1:
                sched.append(("conv", l))
            for j in range(3):
                sched.append(("ffn", l, f, j))
    sched.append(("k",))
    sched.append(("v",))
    for l in range(2, 4):
        for j in range(3):
            sched.append(("ffn", l, 0, j))
        sched.append(("q", l))
        sched.append(("att", l))
        sched.append(("o", l))
        for j in range(3):
            sched.append(("ffn", l, 1, j))
    sched.append(("fin",))
    if stop_after is not None:
        sched = sched[:stop_after]

    def weights_for(item, buf):
        kind = item[0]
        if kind == "ffn":
            return ffn_weights(item[1], item[2], item[3], buf)
        if kind == "conv":
            return conv_weights(item[1], buf)
        if kind == "k":
            return qk_weights(wkv, wksw, buf)
        if kind == "v":
            return v_weights(buf)
        if kind == "q":
            return qk_weights(wq[item[1] - 2], wqsw[item[1] - 2], buf)
        if kind == "o":
            return o_weights(item[1] - 2)
        return [], None

    wviews = [None] * len(sched)
    if sched:
        specs, wviews[0] = weights_for(sched[0], 0)
        queue_weights(specs)
        pump(10 ** 6)
    first_x = True
    for pi, item in enumerate(sched):
        buf = pi % 2
        if pi + 1 < len(sched):
            specs, wviews[pi + 1] = weights_for(sched[pi + 1], (pi + 1) % 2)
            queue_weights(specs)
        kind = item[0]
        wv = wviews[pi]
        if kind == "ffn":
            l, f, j = item[1], item[2], item[3]
            tiles = ALLT if l < 2 else OWNT
            ffn_pass(l, f, j, buf, wv, tiles, src0=(xin if first_x else None))
            first_x = False
        elif kind == "conv":
            conv_pass(item[1], buf, wv, ALLT)
        elif kind == "k":
            qk_pass(buf, wv, KVT, 12, kvm[:, 0:8], KT, 0, "K", True)
        elif kind == "v":
            v_pass(buf, wv, KVT)
        elif kind == "q":
            l = item[1]
            qk_pass(buf, wv, OWNT, l * 3 + 1, mods[:, l, 24:32], QT, HALO, "Q", False)
        elif kind == "att":
            att_pass(buf)
        elif kind == "o":
            o_pass(item[1], wv, OWNT)
        elif kind == "fin":
            fin_pass(OWNT)
    P.barrier()
    P.emit()
    return nc


def host_inputs(x, c, positions, norm_g, ada_w, ada_b, ffn1_w_in, ffn1_w_out, ffn2_w_in, ffn2_w_out,
                conv_w_in, conv_w, conv_w_out, kv_norm_g, kv_ada_w, kv_ada_b, w_kv, attn_w_q, attn_w_o, final_norm_g):
    f32 = np.float32

    def fm(v):
        v = np.asarray(v, f32)
        return v.reshape(v.shape[:-1] + (8, 128))

    perm = np.arange(1536).reshape(24, 64)
    p2 = perm.copy()
    p2[:, 0:8] = perm[:, 8:16]
    p2[:, 8:16] = perm[:, 0:8]
    p2 = p2.reshape(-1)
    w_kv = np.asarray(w_kv, f32)
    attn_w_q = np.asarray(attn_w_q, f32)
    shared = {
        "adaw": np.ascontiguousarray(np.asarray(ada_w, f32)),
        "adab": np.ascontiguousarray(np.asarray(ada_b, f32).reshape(4, 72, 128).transpose(2, 0, 1)),
        "kvadaw": np.ascontiguousarray(np.asarray(kv_ada_w, f32)),
        "kvadab": np.ascontiguousarray(np.asarray(kv_ada_b, f32).reshape(16, 128).T),
        "normg": np.ascontiguousarray(np.asarray(norm_g, f32).reshape(4, 3, 8, 128).transpose(3, 0, 1, 2).reshape(128, 96)),
        "kvnormg": np.ascontiguousarray(np.asarray(kv_norm_g, f32).reshape(8, 128).T),
        "fnormg": np.ascontiguousarray(np.asarray(final_norm_g, f32).reshape(8, 128).T),
        "ffn1_w_in": np.ascontiguousarray(np.asarray(ffn1_w_in, f32)),
        "ffn2_w_in": np.ascontiguousarray(np.asarray(ffn2_w_in, f32)),
        "ffn1_w_out": np.ascontiguousarray(np.asarray(ffn1_w_out, f32)),
        "ffn2_w_out": np.ascontiguousarray(np.asarray(ffn2_w_out, f32)),
        "conv_w_in": np.ascontiguousarray(np.asarray(conv_w_in, f32)),
        "conv_w_out": np.ascontiguousarray(np.asarray(conv_w_out, f32)),
        "convw": np.ascontiguousarray(np.asarray(conv_w, f32).reshape(2, 3, 8, 128).transpose(3, 0, 1, 2).reshape(128, 48)),
        "w_kv": np.ascontiguousarray(w_kv),
        "wk_sw": np.ascontiguousarray(w_kv[:, 0:1536][:, p2]),
        "attn_w_q": np.ascontiguousarray(attn_w_q),
        "wq_sw": np.ascontiguousarray(attn_w_q[:, :, p2]),
        "attn_w_o": np.ascontiguousarray(np.asarray(attn_w_o, f32)),
    }
    qi = np.arange(128)[:, None]
    kj = np.arange(256)[None, :]
    validN = (kj >= qi) & (kj <= qi + 128)
    validF = validN & (kj >= 128)
    maskN = np.where(validN, 0.0, NEG).astype(f32)
    maskFm = np.where(validF, 0.0, NEG).astype(f32)
    sel = np.zeros((128, 128), f32)
    for b in (0, 32, 64):
        sel[b, 0:64] = 1.0
        sel[b + 1, 64:128] = 1.0
    ident = np.eye(128, dtype=f32)
    ropec = np.zeros((128, 2), f32)
    inv = (500000.0 ** (-np.arange(0, 16, 2, dtype=np.float32) / 16)).astype(f32)
    for p in range(128):
        i = p % 64
        if i < 8:
            ropec[p, 0] = inv[i]
            ropec[p, 1] = -1.0
        elif i < 16:
            ropec[p, 0] = inv[i - 8]
            ropec[p, 1] = 1.0
    shared.update({"maskN": maskN, "sel": sel, "ident": ident, "ropec": ropec})
    x = np.asarray(x, f32)
    c = np.asarray(c, f32)
    positions = np.asarray(positions, np.int32)
    in_maps = []
    for core in range(8):
        b, half = core // 2, core % 2
        s0 = half * OWN
        t0 = s0 - HALO
        xr = np.zeros((R, D), f32)
        pr = np.zeros((1, R), np.int32)
        lo = max(t0, 0)
        xr[lo - t0:] = x[b, lo:s0 + OWN]
        pr[0, lo - t0:] = positions[b, lo:s0 + OWN]
        m = dict(shared)
        m["xin"] = np.ascontiguousarray(xr.T.reshape(8, 128, R).transpose(1, 0, 2))
        m["cvec"] = np.ascontiguousarray(c[b].reshape(8, 128).T)
        m["pos"] = pr
        m["flag"] = np.full((128, 1), float(half), f32)
        m["maskF"] = maskFm if half == 0 else maskN
        in_maps.append(m)
    return in_maps


_NC_CACHE = {}


def kernel(**inputs):
    in_maps = host_inputs(**inputs)
    if "nc" not in _NC_CACHE:
        _NC_CACHE["nc"] = build()
    nc = _NC_CACHE["nc"]
    res = run_bass_kernel_spmd(nc, in_maps, core_ids=list(range(8)))
    out = np.zeros((NB, SEQ, D), np.float32)
    for core in range(8):
        b, half = core // 2, core % 2
        o = res.results[core]["out"]
        out[b, half * OWN:(half + 1) * OWN, :] = o.transpose(2, 1, 0).reshape(OWN, D)
    return out
```
